# Optimizing a Trainium2 kernel written in Bass

```python
import math
import jax, jax.numpy as jnp
from jax import lax
import numpy as np

D_MODEL = 2048
BATCH = 16
SEQ = 2048
DEPTH = 1
DEC_BATCH = 16
DEC_SEQ = 16
PAST_LEN = 1024

CHUNK = 64
N_META = 16
N_HEADS = 8
D_HEAD = 64
D_VHEAD = 2 * D_HEAD
QK_W = N_HEADS * D_HEAD
D_ATTN = N_HEADS * D_VHEAD
C_CONV = D_MODEL // 2
CONV_WIDTH = 31
N_BUCKETS = 32
MAX_DISTANCE = 128
Q_BLOCK = 128
EPS = 1e-6
NEG = -1e30
SCALE = D_HEAD ** -0.5
IN_WIDTHS = (QK_W, QK_W, QK_W, QK_W, D_ATTN, D_ATTN, C_CONV, C_CONV, C_CONV, D_MODEL, D_MODEL)
D_IN = sum(IN_WIDTHS)
SPLIT_AT = tuple(int(s) for s in np.cumsum(IN_WIDTHS)[:-1])

kernel_name = 'hybrid_conformer_diffattn_stream_step'


def rmsnorm(x, g):
    x32 = x.astype(jnp.float32)
    y = x32 * lax.rsqrt(jnp.mean(x32 * x32, axis=-1, keepdims=True) + EPS)
    return (y * g.astype(jnp.float32)).astype(x.dtype)


def layernorm(x, g, b):
    x32 = x.astype(jnp.float32)
    mu = jnp.mean(x32, axis=-1, keepdims=True)
    xc = x32 - mu
    y = xc * lax.rsqrt(jnp.mean(xc * xc, axis=-1, keepdims=True) + EPS)
    return (y * g.astype(jnp.float32) + b.astype(jnp.float32)).astype(x.dtype)


def chunk_index(pos):
    return jnp.where(pos < N_META, 0, 1 + (pos - N_META) // CHUNK)


def relative_bias(q_pos, k_pos, table):
    rel = k_pos[None, :] - q_pos[:, None]
    half = N_BUCKETS // 2
    max_exact = half // 2
    n = jnp.abs(rel)
    n_f = jnp.maximum(n, 1).astype(jnp.float32)
    large = max_exact + (jnp.log(n_f / max_exact) / math.log(MAX_DISTANCE / max_exact)
                         * (half - max_exact)).astype(jnp.int32)
    large = jnp.minimum(large, half - 1)
    bucket = jnp.where(rel > 0, half, 0) + jnp.where(n < max_exact, n, large)
    return jnp.transpose(table[bucket], (2, 0, 1)).astype(jnp.float32)


def project_inputs(x, norm_gain, w_in, q_gain, k_gain):
    B, T, _ = x.shape
    u = jnp.einsum('btd,de->bte', rmsnorm(x, norm_gain), w_in)
    q1, q2, k1, k2, v, z_attn, a, g, z_conv, gate_c, gate_a = jnp.split(u, SPLIT_AT, axis=-1)

    def heads(t, gain):
        return rmsnorm(t.reshape(B, T, N_HEADS, D_HEAD), gain)

    q = jnp.concatenate([heads(q1, q_gain), heads(q2, q_gain)], axis=-1).transpose(0, 2, 1, 3)
    k = jnp.concatenate([heads(k1, k_gain), heads(k2, k_gain)], axis=-1).transpose(0, 2, 1, 3)
    v = v.reshape(B, T, N_HEADS, D_VHEAD).transpose(0, 2, 1, 3)
    glu = a * jax.nn.sigmoid(g)
    return q, k, v, z_attn, glu, z_conv, gate_c, gate_a


def diff_attention_block(q, k, v, bias, mask, lam, subln_g, lam_init):
    B, H, Tq, _ = q.shape
    Tk = k.shape[2]
    qm = q.reshape(B, H, Tq, 2, D_HEAD)
    km = k.reshape(B, H, Tk, 2, D_HEAD)
    s = jnp.einsum('bhqmd,bhkmd->bhmqk', qm, km).astype(jnp.float32) * SCALE + bias[None, :, None]
    if mask is not None:
        s = jnp.where(mask, s, NEG)
    p = jax.nn.softmax(s, axis=-1)
    a = p[:, :, 0] - lam * p[:, :, 1]
    o = jnp.einsum('bhqk,bhkd->bhqd', a.astype(v.dtype), v)
    return rmsnorm(o, subln_g) * (1.0 - lam_init)


def prompt_attention(q, k, v, rel_table, lam, subln_g, lam_init):
    B, H, L, _ = q.shape
    n_blk = -(-L // Q_BLOCK)
    L_pad = n_blk * Q_BLOCK
    qb = jnp.pad(q, ((0, 0), (0, 0), (0, L_pad - L), (0, 0)))
    qb = qb.reshape(B, H, n_blk, Q_BLOCK, D_VHEAD).transpose(2, 0, 1, 3, 4)
    k_pos = jnp.arange(L)
    k_chunk = chunk_index(k_pos)

    def one_block(args):
        q_blk, blk = args
        q_pos = blk * Q_BLOCK + jnp.arange(Q_BLOCK)
        bias = relative_bias(q_pos, k_pos, rel_table)
        mask = k_chunk[None, :] <= chunk_index(q_pos)[:, None]
        return diff_attention_block(q_blk, k, v, bias, mask, lam, subln_g, lam_init)

    o = lax.map(one_block, (qb, jnp.arange(n_blk)))
    return o.transpose(1, 2, 0, 3, 4).reshape(B, H, L_pad, D_VHEAD)[:, :, :L]


def causal_depthwise(u_padded, w, b):
    y = lax.conv_general_dilated(u_padded, w[:, None, :].astype(u_padded.dtype), (1,), 'VALID',
                                 dimension_numbers=('NWC', 'WIO', 'NWC'),
                                 feature_group_count=C_CONV)
    return y + b


def finish_layer(x, o_attn, glu_padded, z_attn, z_conv, gate_c, gate_a,
                 conv_w, conv_b, ln_g, ln_b, w_branch_out, w_out):
    B, T, _ = x.shape
    attn = o_attn.transpose(0, 2, 1, 3).reshape(B, T, D_ATTN) * jax.nn.silu(z_attn)
    conv = causal_depthwise(glu_padded, conv_w, conv_b)
    conv = jax.nn.silu(layernorm(conv, ln_g, ln_b)) * jax.nn.silu(z_conv)
    merged = (jax.nn.sigmoid(gate_c) * jnp.einsum('btc,cd->btd', conv, w_branch_out[0])
              + jax.nn.sigmoid(gate_a) * jnp.einsum('btc,cd->btd', attn, w_branch_out[1]))
    return x + jnp.einsum('btd,de->bte', merged, w_out)


def setup_inputs(seed: int = 0) -> dict:
    key = jax.random.key(seed)
    ks = jax.random.split(key, 20)
    f32 = jnp.float32
    nrm = lambda k, shape, s: jax.random.normal(k, shape, f32) * s
    return {
        'x_prompt': nrm(ks[0], (BATCH, SEQ, D_MODEL), 1.0),
        'x_sample': nrm(ks[1], (DEC_BATCH, DEC_SEQ, D_MODEL), 1.0),
        'cache_k': nrm(ks[2], (DEPTH, DEC_BATCH, N_HEADS, PAST_LEN, D_VHEAD), 1.0),
        'cache_v': nrm(ks[3], (DEPTH, DEC_BATCH, N_HEADS, PAST_LEN, D_VHEAD), 1.0),
        'state_conv': nrm(ks[4], (DEPTH, DEC_BATCH, CONV_WIDTH - 1, C_CONV), 0.5),
        'meta_tokens': nrm(ks[5], (N_META, D_MODEL), 1.0),
        'rel_bias': nrm(ks[6], (N_BUCKETS, N_HEADS), 0.5),
        'norm_gain': 1.0 + nrm(ks[7], (DEPTH, D_MODEL), 0.02),
        'w_in': nrm(ks[8], (DEPTH, D_MODEL, D_IN), D_MODEL ** -0.5),
        'q_norm_gain': 1.0 + nrm(ks[9], (DEPTH, N_HEADS, D_HEAD), 0.02),
        'k_norm_gain': 1.0 + nrm(ks[10], (DEPTH, N_HEADS, D_HEAD), 0.02),
        'lambda_qk': nrm(ks[11], (DEPTH, 4, D_HEAD), 0.1),
        'subln_gain': 1.0 + nrm(ks[12], (DEPTH, D_VHEAD), 0.02),
        'conv_w': nrm(ks[13], (DEPTH, CONV_WIDTH, C_CONV), CONV_WIDTH ** -0.5),
        'conv_b': nrm(ks[14], (DEPTH, C_CONV), 0.01),
        'conv_ln_gain': 1.0 + nrm(ks[15], (DEPTH, C_CONV), 0.02),
        'conv_ln_bias': nrm(ks[16], (DEPTH, C_CONV), 0.01),
        'w_branch_out': nrm(ks[17], (DEPTH, 2, C_CONV, D_MODEL), C_CONV ** -0.5),
        'w_out': nrm(ks[18], (DEPTH, D_MODEL, D_MODEL), D_MODEL ** -0.5),
    }


def reference(x_prompt, x_sample, cache_k, cache_v, state_conv, meta_tokens, rel_bias,
              norm_gain, w_in, q_norm_gain, k_norm_gain, lambda_qk, subln_gain,
              conv_w, conv_b, conv_ln_gain, conv_ln_bias, w_branch_out, w_out):
    B = x_prompt.shape[0]
    T_s = x_sample.shape[1]
    past = cache_k.shape[3]
    meta = jnp.broadcast_to(meta_tokens.astype(x_prompt.dtype)[None], (B, N_META, D_MODEL))
    x_p = jnp.concatenate([meta, x_prompt], axis=1)
    x_s = x_sample
    q_pos_s = past + jnp.arange(T_s)
    k_pos_s = jnp.arange(past + T_s)
    k_p_rows, v_p_rows, conv_p_rows, k_s_rows, v_s_rows, conv_s_rows = [], [], [], [], [], []
    for layer in range(DEPTH):
        lam_init = 0.8 - 0.6 * math.exp(-0.3 * layer)
        lq = lambda_qk[layer].astype(jnp.float32)
        lam = jnp.exp(jnp.sum(lq[0] * lq[1])) - jnp.exp(jnp.sum(lq[2] * lq[3])) + lam_init
        q_p, k_p, v_p, za_p, glu_p, zc_p, gc_p, ga_p = project_inputs(
            x_p, norm_gain[layer], w_in[layer], q_norm_gain[layer], k_norm_gain[layer])
        o_p = prompt_attention(q_p, k_p, v_p, rel_bias, lam, subln_gain[layer], lam_init)
        glu_pad_p = jnp.concatenate(
            [jnp.zeros((B, CONV_WIDTH - 1, C_CONV), glu_p.dtype), glu_p], axis=1)
        k_p_rows.append(k_p)
        v_p_rows.append(v_p)
        conv_p_rows.append(glu_pad_p[:, -(CONV_WIDTH - 1):])
        x_p = finish_layer(x_p, o_p, glu_pad_p, za_p, zc_p, gc_p, ga_p, conv_w[layer], conv_b[layer],
                           conv_ln_gain[layer], conv_ln_bias[layer], w_branch_out[layer], w_out[layer])
        q_s, k_s, v_s, za_s, glu_s, zc_s, gc_s, ga_s = project_inputs(
            x_s, norm_gain[layer], w_in[layer], q_norm_gain[layer], k_norm_gain[layer])
        k_all = jnp.concatenate([cache_k[layer].astype(k_s.dtype), k_s], axis=2)
        v_all = jnp.concatenate([cache_v[layer].astype(v_s.dtype), v_s], axis=2)
        bias_s = relative_bias(q_pos_s, k_pos_s, rel_bias)
        o_s = diff_attention_block(q_s, k_all, v_all, bias_s, None, lam, subln_gain[layer], lam_init)
        glu_pad_s = jnp.concatenate([state_conv[layer].astype(glu_s.dtype), glu_s], axis=1)
        k_s_rows.append(k_s)
        v_s_rows.append(v_s)
        conv_s_rows.append(glu_pad_s[:, -(CONV_WIDTH - 1):])
        x_s = finish_layer(x_s, o_s, glu_pad_s, za_s, zc_s, gc_s, ga_s, conv_w[layer], conv_b[layer],
                           conv_ln_gain[layer], conv_ln_bias[layer], w_branch_out[layer], w_out[layer])
    y_prompt = x_p[:, N_META:]
    y_sample = x_s
    k_prompt = jnp.stack(k_p_rows)
    v_prompt = jnp.stack(v_p_rows)
    conv_prompt = jnp.stack(conv_p_rows)
    k_sample = jnp.stack(k_s_rows)
    v_sample = jnp.stack(v_s_rows)
    conv_sample = jnp.stack(conv_s_rows)
    return (y_prompt, y_sample, k_prompt, v_prompt, conv_prompt, k_sample, v_sample, conv_sample)
```

```python
import math
import os
import numpy as np
import concourse.bass as bass
import concourse.mybir as mybir
from concourse.bass_utils import run_bass_kernel_spmd

F32 = mybir.dt.float32
BF16 = mybir.dt.bfloat16
U8 = mybir.dt.uint8
AF = mybir.ActivationFunctionType
ALU = mybir.AluOpType
AX = mybir.AxisListType

NCORES = 8
D = 2048
SEQ = 2048
NMETA = 16
L = SEQ + NMETA
DIN = 11264
H = 8
PAST = 1024
TS_ = 16
CW = 31
EPS = 1e-6
SCALE = 0.125
LAM_INIT = 0.8 - 0.6 * math.exp(0.0)
NCMAX = 1072
LAYOUT = {}


def _bucket(r):
    n = abs(int(r))
    nf = np.float32(max(n, 1))
    large = 8 + int(np.float32(np.log(nf / np.float32(8))) / np.float32(math.log(16.0)) * np.float32(8))
    large = min(large, 15)
    return (16 if r > 0 else 0) + (n if n < 8 else large)


class Prog:
    def __init__(self, nc):
        self.nc = nc
        self.engs = {"pe": nc.tensor, "act": nc.scalar, "dve": nc.vector, "pool": nc.gpsimd, "sp": nc.sync}
        self.q = {k: [] for k in self.engs}
        self.sems = {}
        self.cnt = {}
        self.waited = {k: {} for k in self.engs}
        self.lastw = {}
        self.readers = {}
        self.stack = None

    def sem(self, name):
        if name not in self.sems:
            self.sems[name] = self.stack.enter_context(self.nc.semaphore(name))
            self.cnt[name] = 0
        return name

    def _wait(self, eng, tok):
        name, val, prod = tok
        if prod == "pe" and eng == "pe":
            return
        if self.waited[eng].get(name, 0) >= val:
            return
        self.waited[eng][name] = val
        self.q[eng].append(("wait", name, val))

    def _deps(self, eng, reads, writes):
        toks = []
        for k in reads:
            if k in self.lastw:
                toks.append(self.lastw[k])
        for k in writes:
            if k in self.lastw:
                toks.append(self.lastw[k])
            toks.extend(self.readers.get(k, ()))
        for t in toks:
            self._wait(eng, t)

    def _commit(self, tok, reads, writes):
        for k in reads:
            self.readers.setdefault(k, []).append(tok)
        for k in writes:
            self.lastw[k] = tok
            self.readers[k] = []

    def op(self, eng, fns, reads=(), writes=()):
        if not isinstance(fns, (list, tuple)):
            fns = [fns]
        self._deps(eng, reads, writes)
        name = self.sem("e_" + eng)
        self.cnt[name] += 1
        tok = (name, self.cnt[name], eng)
        for f in fns[:-1]:
            self.q[eng].append(("op", f, None))
        self.q[eng].append(("op", fns[-1], name))
        self._commit(tok, reads, writes)
        return tok

    def dma(self, eng, out, in_, sem, reads=(), writes=(), **kw):
        self._deps(eng, reads, writes)
        name = self.sem(sem)
        self.cnt[name] += 16
        tok = (name, self.cnt[name], "dma")
        self.q[eng].append(("dma", out, in_, name, kw))
        self._commit(tok, reads, writes)
        return tok

    def barrier(self):
        for eng in self.engs:
            for name, c in self.cnt.items():
                if c > 0:
                    self._wait(eng, (name, c, "x"))
        self.lastw = {}
        self.readers = {}

    def emit(self, block):
        def run(qname):
            def f(e):
                for it in self.q[qname]:
                    if it[0] == "wait":
                        e.wait_ge(self.sems[it[1]], it[2])
                    elif it[0] == "op":
                        ins = it[1](e)
                        if it[2] is not None:
                            ins.then_inc(self.sems[it[2]], 1)
                    else:
                        e.dma_start(out=it[1], in_=it[2], **it[4]).then_inc(self.sems[it[3]], 16)
            return f
        block.tensor(run("pe"))
        block.scalar(run("act"))
        block.vector(run("dve"))
        block.gpsimd(run("pool"))
        block.sync(run("sp"))


def MM(out, lhsT, rhs, start=True, stop=True):
    return lambda e: e.matmul(out, lhsT=lhsT, rhs=rhs, start=start, stop=stop)


def TR(out, in_, ident):
    return lambda e: e.transpose(out=out, in_=in_, identity=ident)


def ACT(out, in_, func, bias=None, scale=None, accum=None):
    kw = {}
    if bias is not None:
        kw["bias"] = bias
    if scale is not None:
        kw["scale"] = scale
    if accum is not None:
        kw["accum_out"] = accum
    return lambda e: e.activation(out=out, in_=in_, func=func, **kw)


def TS(out, in0, s1, op0, s2=None, op1=None):
    if op1 is None:
        return lambda e: e.tensor_scalar(out=out, in0=in0, scalar1=s1, scalar2=None, op0=op0)
    return lambda e: e.tensor_scalar(out=out, in0=in0, scalar1=s1, scalar2=s2, op0=op0, op1=op1)


def STT(out, in0, scalar, in1, op0, op1):
    return lambda e: e.scalar_tensor_tensor(out=out, in0=in0, scalar=scalar, in1=in1, op0=op0, op1=op1)


def TT(out, in0, in1, op):
    return lambda e: e.tensor_tensor(out=out, in0=in0, in1=in1, op=op)


def CP(out, in_):
    return lambda e: e.tensor_copy(out=out, in_=in_)


def RCP(out, in_):
    return lambda e: e.reciprocal(out=out, in_=in_)


def MS(ap, v):
    return lambda e: e.memset(ap, v)


def build_program(stack):
    nc = bass.Bass("TRN2", target_bir_lowering=False)
    P = Prog(nc)
    P.stack = stack

    def din(name, shape):
        return nc.dram_tensor(name, list(shape), F32, kind="ExternalInput").ap()

    def dout(name, shape):
        return nc.dram_tensor(name, list(shape), F32, kind="ExternalOutput").ap()

    x_prompt = din("x_prompt", [2, SEQ, D])
    x_sample = din("x_sample", [2, TS_, D])
    cache_k = din("cache_k", [2, H, PAST, 128])
    cache_v = din("cache_v", [2, H, PAST, 128])
    state_conv = din("state_conv", [2, 30, 1024])
    meta_tokens = din("meta_tokens", [NMETA, D])
    rel_bias = din("rel_bias", [32, H])
    norm_gain = din("norm_gain", [1, D])
    w_in = din("w_in", [D, DIN])
    q_norm_gain = din("q_norm_gain", [H, 64])
    k_norm_gain = din("k_norm_gain", [H, 64])
    lambda_qk = din("lambda_qk", [1, 256])
    subln_gain = din("subln_gain", [128, 1])
    conv_w = din("conv_w", [CW, 1024])
    conv_b = din("conv_b", [1024])
    conv_ln_gain = din("conv_ln_gain", [1024])
    conv_ln_bias = din("conv_ln_bias", [1024])
    w_bo = din("w_branch_out", [2048, D])
    w_out = din("w_out", [D, D])
    c_ident = din("c_ident", [128, 128])
    c_oh = din("c_oh", [32, 383])

    y_prompt = dout("y_prompt", [2, SEQ, D])
    y_sample = dout("y_sample", [2, TS_, D])
    k_prompt = dout("k_prompt", [2, H, L, 128])
    v_prompt = dout("v_prompt", [2, H, L, 128])
    conv_prompt = dout("conv_prompt", [2, 30, 1024])
    k_sample = dout("k_sample", [2, H, TS_, 128])
    v_sample = dout("v_sample", [2, H, TS_, 128])
    conv_sample = dout("conv_sample", [2, 30, 1024])
    gscr = nc.dram_tensor("gscr", [H, 383], F32, kind="Internal").ap()

    ARENA = 212000
    arena = stack.enter_context(nc.sbuf_tensor("arena", [128, ARENA], U8))
    ps = stack.enter_context(nc.psum_tensor("ps", [128, 8, 512], F32))
    cur = [0]

    def carve(nbytes):
        o = cur[0]
        cur[0] += (nbytes + 31) // 32 * 32
        assert cur[0] <= ARENA, cur[0]
        return o

    def view(off, dt, shape):
        esz = 4 if dt == F32 else 2
        n = int(np.prod(shape))
        v = arena[:, off:off + n * esz].bitcast(dt)
        if len(shape) == 1:
            return v
        names = " ".join("d%d" % i for i in range(len(shape)))
        kw = {"d%d" % i: shape[i] for i in range(len(shape))}
        return v.rearrange("p (%s) -> p %s" % (names, names), **kw)

    def alloc(dt, shape):
        esz = 4 if dt == F32 else 2
        return view(carve(int(np.prod(shape)) * esz), dt, shape)

    R1 = carve(16 * NCMAX * 2)
    R2 = carve(8 * 1040 * 2 + 9 * 8 * 128 * 2)
    R3 = carve(34048)
    R4 = carve(16 * NCMAX * 2)
    R5 = carve(8 * NCMAX * 2)
    hT = view(R1, BF16, [16, NCMAX])
    kT_a = view(R2, BF16, [8, 1040])
    Vt_a = view(R2 + 8 * 1040 * 2, BF16, [9, 8, 128])
    kT_b = view(R3, BF16, [8, 1024])
    Vt_b = view(R3 + 16384, BF16, [8, 8, 128])
    GW = 30 + 1024 + 2
    glu = view(R3, BF16, [8, GW])
    convB = view(R3 + 8 * GW * 2, BF16, [8, NCMAX])
    wout_s = [view(R3 + i * 16384, BF16, [16, 512]) for i in range(2)]
    xt_s = [view(R3 + i * 8192, F32, [2048]) for i in range(2)]
    hb_s = [view(R3 + 16384 + i * 4096, BF16, [2048]) for i in range(2)]
    qT = view(R4, BF16, [8, NCMAX])
    R4b = R4 + 8 * NCMAX * 2
    merged = view(R4, BF16, [16, NCMAX])
    attnB = view(R5, BF16, [8, NCMAX])
    wslot = [alloc(BF16, [16, 128]) for _ in range(3)]
    Ehl = alloc(BF16, [2, 8, 256])
    gain_b = alloc(F32, [2048])
    ident = alloc(F32, [128])
    identb = alloc(BF16, [128])
    blk64 = alloc(BF16, [128])
    onesb = alloc(BF16, [128])
    ones_ln = alloc(F32, [128])
    ones_lnb = alloc(BF16, [128])
    ones_d = alloc(F32, [128])
    gq = alloc(F32, [8])
    gk = alloc(F32, [8])
    subg = alloc(F32, [1])
    cb = alloc(F32, [8])
    lng = alloc(F32, [8])
    lnb = alloc(F32, [8])
    cw = alloc(F32, [8, CW])
    c15 = alloc(F32, [8])
    neglam = alloc(F32, [1])
    gh = alloc(BF16, [8, 30])
    gm = alloc(BF16, [8, 16])
    small = alloc(F32, [16])
    TMP = carve(12288)
    SAMP = carve(5632)
    LAYOUT.update(R1=R1, R2=R2, R3=R3, R4=R4, R5=R5, GW=GW)
    print("arena used", cur[0])

    bank = lambda i: ps[:, i, :]

    wq = []
    wstate = {"emitted": 0, "cur": 0, "nblk": 0, "nwout": 0, "wout_ok": -1}

    def w_in_blk(col):
        if isinstance(col, tuple):
            c = col[1]
            return [w_in[:, c + 512 * t:c + 512 * t + 64].rearrange("(kc p) c -> p kc c", p=128) for t in range(2)]
        return w_in[:, col:col + 128].rearrange("(kc p) c -> p kc c", p=128)

    slot_last = {"blk": [-1, -1, -1], "wout": [-1, -1]}

    def emit_w(icur):
        while wstate["emitted"] < min(icur + 3, len(wq)):
            idx = wstate["emitted"]
            ent = wq[idx]
            if ent["kind"] == "wout":
                if ent["seg"] > wstate["wout_ok"]:
                    return
                s = wstate["nwout"] % 2
                if slot_last["wout"][s] >= icur:
                    return
                slot_last["wout"][s] = idx
                wstate["nwout"] += 1
                ent["ap"] = wout_s[s]
                ent["key"] = ("wout", s)
                P.dma("pool", wout_s[s], ent["src"], "wo%d" % s, writes=[ent["key"]])
            else:
                s = wstate["nblk"] % 3
                if slot_last["blk"][s] >= icur:
                    return
                slot_last["blk"][s] = idx
                wstate["nblk"] += 1
                ent["ap"] = wslot[s]
                ent["key"] = ("w", s)
                if isinstance(ent["src"], list):
                    for t, src in enumerate(ent["src"]):
                        P.dma("pool", wslot[s][:, :, 64 * t:64 * t + 64], src, "w%d" % s, writes=[ent["key"]])
                else:
                    P.dma("pool", wslot[s], ent["src"], "w%d" % s, writes=[ent["key"]])
            wstate["emitted"] += 1

    def next_w():
        i = wstate["cur"]
        emit_w(i)
        assert wstate["emitted"] > i, "weight not emitted"
        wstate["cur"] += 1
        return wq[i]["ap"], wq[i]["key"]

    OVERLAP0 = not os.environ.get('KNO_OVERLAP0')
    EARLY_GATES = not os.environ.get('KNO_EARLY')
    stop = os.environ.get("KSTOP", "")

    def stop_here(sg):
        return stop == "%d:D" % sg["si"]

    segs = []
    for bl in range(2):
        segs.append(dict(si=2 * bl, b=bl, hf=0, next=48 if bl == 0 else 0, samp=(bl == 0)))
        segs.append(dict(si=2 * bl + 1, b=bl, hf=1, next=0, samp=False))

    A1_blocks = []
    for hh_ in range(8):
        A1_blocks += [("k", hh_, ("pair", 1024 + 64 * hh_)), ("q", hh_, ("pair", 64 * hh_))]
    for j in range(8):
        A1_blocks += [("v", j, 2048 + 128 * j), ("z", j, 3072 + 128 * j)]
    B_blocks = []
    for g in range(8):
        B_blocks += [("a", g, 4096 + 128 * g), ("g", g, 5120 + 128 * g)]
    for g in range(8):
        B_blocks += [("zc", g, 6144 + 128 * g)]
    for sg in segs:
        for (_, _, col) in A1_blocks + B_blocks:
            wq.append(dict(kind="blk", src=w_in_blk(col)))
        for f in range(16):
            wq.append(dict(kind="blk", src=w_in_blk(7168 + 128 * f)))
            wq.append(dict(kind="blk", src=w_in_blk(9216 + 128 * f)))
            wq.append(dict(kind="blk", src=w_bo[:, 128 * f:128 * f + 128].rearrange("(kc p) c -> p kc c", p=128)))
        for cg in range(4):
            wq.append(dict(kind="wout", seg=sg["si"],
                           src=w_out[:, 512 * cg:512 * cg + 512].rearrange("(kc p) c -> p kc c", p=128)))

    pend = []

    def defer(delay, fn):
        pend.append([delay, fn])

    def tick():
        todo = [p for p in pend if p[0] <= 0]
        for p in todo:
            pend.remove(p)
        for p in pend:
            p[0] -= 1
        for p in todo:
            p[1]()

    def flush():
        while pend:
            tick()

    mb = [0]

    def main_bank():
        mb[0] = (mb[0] + 1) % 4
        return mb[0]

    ab = [0]

    def aux_bank():
        ab[0] = (ab[0] + 1) % 4
        return 4 + ab[0]

    rot = {}

    def rslot(name, n):
        rot[name] = (rot.get(name, -1) + 1) % n
        return rot[name]

    def bcast_rows(src_ap_tensor, offset, n):
        return bass.AP(tensor=src_ap_tensor, offset=offset, ap=[[0, 128], [1, n]])

    P.dma("sp", ident, c_ident, "cst", writes=["ident"])
    P.dma("sp", gain_b, bcast_rows(norm_gain.tensor, 0, 2048), "cst", writes=["gain_b"])
    P.dma("sp", c15, bcast_rows(rel_bias.tensor, 15 * H, H), "cst", writes=["c15"])
    for two in range(2):
        P.dma("sp", gq[64 * two:64 * two + 64, :], q_norm_gain.rearrange("h d -> d h"), "cst", writes=[("gq", two)],
              allow_slow_non_contiguous=True)
        P.dma("sp", gk[64 * two:64 * two + 64, :], k_norm_gain.rearrange("h d -> d h"), "cst", writes=[("gk", two)],
              allow_slow_non_contiguous=True)
    P.dma("sp", subg, subln_gain, "cst", writes=["subg"])
    P.dma("sp", cb, conv_b.rearrange("(g p) -> p g", p=128), "cst", writes=["cb"], allow_slow_non_contiguous=True)
    P.dma("sp", lng, conv_ln_gain.rearrange("(g p) -> p g", p=128), "cst", writes=["lng"],
          allow_slow_non_contiguous=True)
    P.dma("sp", lnb, conv_ln_bias.rearrange("(g p) -> p g", p=128), "cst", writes=["lnb"],
          allow_slow_non_contiguous=True)
    lq = view(R4, F32, [256])
    P.dma("sp", lq, bcast_rows(lambda_qk.tensor, 0, 256), "cst", writes=["lq"])
    cwr = view(R4 + 1024, F32, [1024])
    P.dma("sp", cwr[0:CW, :], conv_w, "cst", writes=["cwr"])
    tbl = view(R4 + 5120, F32, [8])
    P.dma("sp", tbl[0:32, :], rel_bias, "cst", writes=["tbl"])
    ohs = view(R4 + 5200, F32, [383])
    P.dma("sp", ohs[0:32, :], c_oh, "cst", writes=["ohs"])
    P.barrier()
    P.op("dve", CP(identb, ident), reads=["ident"], writes=["identb"])
    P.op("dve", MS(blk64, 0.0), writes=["blk64"])
    P.op("dve", MS(blk64[0:64, 0:64], 1.0 / 64), writes=["blk64"])
    P.op("dve", MS(blk64[64:128, 64:128], 1.0 / 64), writes=["blk64"])
    P.op("dve", MS(onesb, 1.0), writes=["onesb"])
    P.op("dve", MS(ones_ln, 1.0 / 1024), writes=["ones_ln"])
    P.op("dve", MS(ones_lnb, 1.0 / 1024), writes=["ones_lnb"])
    P.op("dve", MS(ones_d, 1.0 / 128), writes=["ones_d"])
    P.op("dve", MS(gh, 0.0), writes=["gh"])
    P.op("dve", TS(subg, subg, 1.0 - LAM_INIT, ALU.mult), reads=["subg"], writes=["subg"])
    l1 = view(R4 + 8192, F32, [64])
    l2 = view(R4 + 8192 + 256, F32, [64])
    P.op("dve", TT(l1, lq[:, 0:64], lq[:, 64:128], ALU.mult), reads=["lq"], writes=["l1"])
    P.op("dve", TT(l2, lq[:, 128:192], lq[:, 192:256], ALU.mult), reads=["lq"], writes=["l2"])
    P.op("dve", lambda e: e.reduce_sum(out=small[:, 0:1], in_=l1, axis=AX.X), reads=["l1"], writes=["s0"])
    P.op("dve", lambda e: e.reduce_sum(out=small[:, 1:2], in_=l2, axis=AX.X), reads=["l2"], writes=["s1"])
    P.op("act", ACT(small[:, 2:4], small[:, 0:2], AF.Exp), reads=["s0", "s1"], writes=["s2"])
    P.op("dve", TT(small[:, 4:5], small[:, 3:4], small[:, 2:3], ALU.subtract), reads=["s2"], writes=["s4"])
    P.op("dve", TS(neglam, small[:, 4:5], -LAM_INIT, ALU.add), reads=["s4"], writes=["neglam"])
    for g in range(8):
        P.op("pe", TR(ps[:, 4, g * 32:g * 32 + CW], cwr[0:CW, g * 128:(g + 1) * 128], ident[0:CW, 0:CW]),
             reads=["cwr", "ident"], writes=[("ps", 4)])
    P.op("dve", CP(cw, ps[:, 4, 0:256].rearrange("p (g j) -> p g j", g=8)[:, :, 0:CW]), reads=[("ps", 4)],
         writes=["cw"])
    P.op("pe", MM(ps[0:8, 5, 0:383], tbl[0:32, :], ohs[0:32, :]), reads=["tbl", "ohs"], writes=[("ps", 5)])
    gsb = view(R4 + 8192 + 1024, F32, [383])
    P.op("dve", CP(gsb[0:8, :], ps[0:8, 5, 0:383]), reads=[("ps", 5)], writes=["gsb"])
    P.dma("sp", gscr, gsb[0:8, :], "cst", reads=["gsb"], writes=["gscr"])
    P.barrier()
    erev = view(R1, F32, [8, 256])
    for h in range(H):
        P.dma("sp", erev[:, h, :], bass.AP(tensor=gscr.tensor, offset=h * 383, ap=[[1, 128], [1, 256]]), "cst",
              reads=["gscr"], writes=[("erev", h)])
    P.barrier()
    E = view(R5, F32, [8, 256])
    for h in range(H):
        b0 = erev[:, h, :]
        rv = bass.AP(tensor=b0.tensor, offset=b0.offset + 255, ap=[list(b0.ap[0]), [-1, 256]])
        P.op("dve", CP(E[:, h, :], rv), reads=[("erev", h)], writes=["E"])
    P.op("dve", MS(E[64:128, :, 0:64], -30000.0), writes=["E"])
    for h in range(H):
        P.op("dve", TS(E[:, h, :], E[:, h, :], c15[:, h:h + 1], ALU.subtract, 1.0 / SCALE, ALU.mult), reads=["c15"],
             writes=["E"])
    P.op("dve", CP(Ehl[:, 0], E), writes=["E", "Ehl"])
    P.op("dve", TT(E, E, Ehl[:, 0], ALU.subtract), writes=["E"])
    P.op("dve", CP(Ehl[:, 1], E), writes=["E", "Ehl1"])
    P.barrier()

    def coltiles(sg):
        ct = [(0, 512), (512, 512)]
        if sg["next"]:
            ct.append((1024, sg["next"]))
        return ct

    ss_d = alloc(F32, [4])

    def phase0_steps(sg, overlapped=False):
        b = sg["b"]
        tiles = [(i, 128) for i in range(8)]
        if sg["next"]:
            tiles.append((8, sg["next"]))
        steps = []
        for (ti, n) in tiles:
            def step(ti=ti, n=n):
                if overlapped:
                    s = 0
                    xt, hb = view(TMP, F32, [2048]), view(TMP + 8192, BF16, [2048])
                    ss = ss_d
                    kx, semn = "xD", "xtD"
                    dq = "act"
                else:
                    s = rslot("xt", 2)
                    xt, hb = xt_s[s], hb_s[s]
                    ss = view(TMP + 64 * s, F32, [4])
                    kx, semn = "x", "xt%d" % s
                    dq = "sp"
                if ti < 8:
                    t0 = 1024 * sg["hf"] + 128 * ti
                    P.dma(dq, xt, x_prompt[b, t0:t0 + 128, :], semn, writes=[(kx + "t", s)])
                else:
                    P.dma(dq, xt[0:16, :], meta_tokens, semn, writes=[(kx + "t", s)])
                    if sg["samp"]:
                        P.dma(dq, xt[16:32, :], x_sample[0], semn, writes=[(kx + "t", s)])
                        P.dma(dq, xt[32:48, :], x_sample[1], semn, writes=[(kx + "t", s)])
                P.op("dve", MS(ss[0:n, 0:1], 0.0), writes=[(kx + "ss", s)])
                P.op("act", ACT(hb[0:n, :], xt[0:n, :], AF.Square, accum=ss[0:n, 0:1]), reads=[(kx + "t", s)],
                     writes=[(kx + "hb", s), (kx + "ss", s)])
                P.op("act", ACT(ss[0:n, 1:2], ss[0:n, 0:1], AF.Ln, bias=EPS, scale=1.0 / D), reads=[(kx + "ss", s)],
                     writes=[(kx + "ss1", s)])
                P.op("act", ACT(ss[0:n, 2:3], ss[0:n, 1:2], AF.Exp, scale=-0.5), reads=[(kx + "ss1", s)],
                     writes=[(kx + "ss2", s)])
                P.op("dve", STT(hb[0:n, :], xt[0:n, :], ss[0:n, 2:3], gain_b[0:n, :], ALU.mult, ALU.mult),
                     reads=[(kx + "t", s), (kx + "ss2", s)], writes=[(kx + "hb", s)])
            def step_b(ti=ti, n=n):
                if overlapped:
                    s = 0
                    hb = view(TMP + 8192, BF16, [2048])
                    kx = "xD"
                else:
                    s = rot["xt"]
                    hb = hb_s[s]
                    kx = "x"
                c0 = 128 * ti
                for g4 in range(4):
                    bk = main_bank()
                    P.op("pe", [MM(ps[:, bk, 128 * k:128 * k + n], hb[0:n, (4 * g4 + k) * 128:(4 * g4 + k + 1) * 128],
                                   identb[0:n, 0:n]) for k in range(4)],
                         reads=[(kx + "hb", s)], writes=[("ps", bk)])
                    src = ps[:, bk, :].rearrange("p (a c) -> p a c", a=4)[:, :, 0:n]
                    eng = "dve" if g4 % 2 == 0 else "act"
                    fn = CP(hT[:, 4 * g4:4 * g4 + 4, c0:c0 + n], src) if eng == "dve" else \
                        ACT(hT[:, 4 * g4:4 * g4 + 4, c0:c0 + n], src, AF.Copy)
                    P.op(eng, fn, reads=[("ps", bk)], writes=[("hT", ti)])
            steps.append(step)
            steps.append(step_b)
        return steps

    def phase0(sg):
        if sg["si"] > 0 and OVERLAP0:
            return
        for st in phase0_steps(sg):
            st()

    def inproj_group(wap, wkey, c0, n, nk=16, rhs=None, koff=0):
        bk = main_bank()
        rhs = rhs or (lambda kc: hT[:, kc, c0:c0 + n])
        P.op("pe", [MM(ps[:, bk, 0:n], wap[:, koff + kc, :], rhs(kc), start=(kc == 0), stop=(kc == nk - 1))
                    for kc in range(nk)],
             reads=[wkey, "hTall"], writes=[("ps", bk)])
        tick()
        return bk

    def phaseA1(sg):
        b, hf = sg["b"], sg["hf"]
        kst = view(R4b, F32, [9, 128])
        vst = view(R4b + 9 * 2 * 128 * 4, F32, [9, 128])
        pos0 = NMETA + 1024 * hf
        import os
        nb_ = int(os.environ.get('KBLK', 99))
        for bi_, (typ, j, col) in enumerate(A1_blocks):
            if bi_ >= nb_:
                break
            wap, wkey = next_w()
            for ci, (c0, n) in enumerate(coltiles(sg)):
                bk = inproj_group(wap, wkey, c0, n)
                if typ in ("q", "k"):
                    s = rslot("sq", 2)
                    sqb = view(TMP + 1024 * s, BF16, [512])
                    rs = view(TMP + 2048 + 2048 * s, F32, [512])
                    P.op("act", ACT(sqb[:, 0:n], ps[:, bk, 0:n], AF.Square), reads=[("ps", bk)], writes=[("sqb", s)])

                    def st1(typ=typ, j=j, ci=ci, c0=c0, n=n, bk=bk, s=s, sqb=sqb, rs=rs):
                        ax = aux_bank()
                        P.op("pe", MM(ps[:, ax, 0:n], blk64, sqb[:, 0:n]), reads=[("sqb", s)], writes=[("ps", ax)])
                        P.op("act", ACT(rs[:, 0:n], ps[:, ax, 0:n], AF.Ln, bias=EPS, scale=1.0),
                             reads=[("ps", ax)], writes=[("rs", s)])
                        P.op("act", ACT(rs[:, 0:n], rs[:, 0:n], AF.Exp, scale=-0.5), reads=[("rs", s)],
                             writes=[("rs", s)])
                        if typ == "q":
                            P.op("dve", STT(qT[:, j, c0:c0 + n], ps[:, bk, 0:n], gq[:, j:j + 1], rs[:, 0:n],
                                            ALU.mult, ALU.mult),
                                 reads=[("ps", bk), ("rs", s)], writes=[("qT", j, ci)])
                            return
                        s2 = rslot("kn", 2)
                        kn = view(TMP + 6144 + 2048 * s2, F32, [512])
                        P.op("dve", STT(kn[:, 0:n], ps[:, bk, 0:n], gk[:, j:j + 1], rs[:, 0:n],
                                        ALU.mult, ALU.mult),
                             reads=[("ps", bk), ("rs", s)], writes=[("kn", s2)])
                        if ci < 2:
                            dst = (kT_b if hf else kT_a)[:, j, c0:c0 + n]
                            P.op("act", ACT(dst, kn[:, 0:n], AF.Copy), reads=[("kn", s2)], writes=[("kT", j, ci)])
                        else:
                            P.op("act", ACT(kT_a[:, j, 1024:1040], kn[:, 0:16], AF.Copy), reads=[("kn", s2)],
                                 writes=[("kT", j, ci)])
                            if sg["samp"]:
                                P.op("act", ACT(ksamp[:, j, :], kn[:, 16:48], AF.Copy), reads=[("kn", s2)],
                                     writes=[("kTs", j)])

                        def st2():
                            ax2 = aux_bank()
                            if ci < 2:
                                P.op("pe", [TR(ps[:, ax2, 128 * k:128 * k + 128], kn[:, 128 * k:128 * k + 128], ident)
                                            for k in range(4)], reads=[("kn", s2)], writes=[("ps", ax2)])
                                P.op("dve", CP(kst[:, 4 * ci:4 * ci + 4, :],
                                               ps[:, ax2, :].rearrange("p (s d) -> p s d", s=4)),
                                     reads=[("ps", ax2)], writes=[("kst", ci)])
                            else:
                                P.op("pe", TR(ps[0:n, ax2, 0:128], kn[:, 0:n], ident), reads=[("kn", s2)],
                                     writes=[("ps", ax2)])
                                P.op("dve", CP(kst[0:n, 8, :], ps[0:n, ax2, 0:128]),
                                     reads=[("ps", ax2)], writes=[("kst", ci)])
                            if ci == len(coltiles(sg)) - 1:
                                rk = [("kst", c) for c in range(3)]
                                P.dma("sp", k_prompt[b, j, pos0:pos0 + 1024, :].rearrange("(t p) f -> p t f", p=128),
                                      kst[:, 0:8, :], "kst", reads=rk, writes=rk)
                                if sg["next"]:
                                    for b2_ in range(2):
                                        P.dma("sp", k_prompt[b2_, j, 0:16, :], kst[0:16, 8, :], "kst", reads=rk,
                                              writes=rk)
                                if sg["samp"]:
                                    for sq_ in range(2):
                                        P.dma("sp", k_sample[sq_, j, :, :], kst[16 + 16 * sq_:32 + 16 * sq_, 8, :],
                                              "kst", reads=rk, writes=rk)
                        defer(1, st2)
                    defer(1, st1)
                elif typ == "v":
                    s = rslot("vf", 2)
                    vf = view(TMP + 6144 + 2048 * s, F32, [512])
                    P.op("act", ACT(vf[:, 0:n], ps[:, bk, 0:n], AF.Copy), reads=[("ps", bk)], writes=[("kn", s)])

                    def st1(j=j, ci=ci, c0=c0, n=n, s=s, vf=vf):
                        ax = aux_bank()
                        kvm = int(os.environ.get('KV_MODE', 9))
                        if kvm < 1:
                            return
                        if ci < 2:
                            P.op("pe", [TR(ps[:, ax, 128 * k:128 * k + 128], vf[:, 128 * k:128 * k + 128], ident)
                                        for k in range(4)], reads=[("kn", s)], writes=[("ps", ax)])
                            src = ps[:, ax, :].rearrange("p (s d) -> p s d", s=4)
                            if kvm < 2:
                                return
                            P.op("dve", CP(vst[:, 4 * ci:4 * ci + 4, :], src), reads=[("ps", ax)],
                                 writes=[("vst", ci)])
                            if kvm < 3:
                                return
                            Vt = Vt_b if hf else Vt_a
                            P.op("act", ACT(Vt[:, 4 * ci:4 * ci + 4, j, :], vst[:, 4 * ci:4 * ci + 4, :], AF.Copy),
                                 reads=[("vst", ci)], writes=[("Vt", j, ci)])
                        else:
                            if kvm < 4:
                                return
                            P.op("pe", TR(ps[0:n, ax, 0:128], vf[:, 0:n], ident), reads=[("kn", s)],
                                 writes=[("ps", ax)])
                            P.op("dve", CP(vst[0:n, 8, :], ps[0:n, ax, 0:128]), reads=[("ps", ax)],
                                 writes=[("vst", ci)])
                            P.op("act", ACT(Vt_a[0:16, 8, j, :], vst[0:16, 8, :], AF.Copy), reads=[("vst", ci)],
                                 writes=[("Vt", j, ci)])
                            if sg["samp"] and not os.environ.get('KV_NOVNS'):
                                vns = view(SAMP, BF16, [2, 8, 128])
                                for sq_ in range(2):
                                    P.op("pe", TR(ps[0:16, ax, 128 + 128 * sq_:256 + 128 * sq_],
                                                  vf[:, 16 + 16 * sq_:32 + 16 * sq_], ident),
                                         reads=[("kn", s)], writes=[("ps", ax)])
                                P.op("act", ACT(vns[0:16, :, j, :],
                                                ps[0:16, ax, 128:384].rearrange("p (s d) -> p s d", s=2), AF.Copy),
                                     reads=[("ps", ax)], writes=[("vns", j)])
                        if ci == len(coltiles(sg)) - 1 and not os.environ.get('KV_NODMA'):
                            rk = [("vst", c) for c in range(3)]
                            P.dma("sp", v_prompt[b, j, pos0:pos0 + 1024, :].rearrange("(t p) f -> p t f", p=128),
                                  vst[:, 0:8, :], "vst", reads=rk, writes=rk)
                            if sg["next"]:
                                for b2_ in range(2):
                                    P.dma("sp", v_prompt[b2_, j, 0:16, :], vst[0:16, 8, :], "vst", reads=rk, writes=rk)
                            if sg["samp"]:
                                for sq_ in range(2):
                                    P.dma("sp", v_sample[sq_, j, :, :], vst[16 + 16 * sq_:32 + 16 * sq_, 8, :], "vst",
                                          reads=rk, writes=rk)
                    defer(1, st1)
                else:
                    P.op("act", ACT(attnB[:, j, c0:c0 + n], ps[:, bk, 0:n], AF.Silu), reads=[("ps", bk)],
                         writes=[("attnB", j, ci)])
        flush()

    def attn_job(h, qsrc, ncols, ktiles, ob, zb):
        prev = [None]

        def av(items, first, last):
            for (m, pt, nk, c0, V, key) in items:
                P.op("pe", MM(ps[:, ob[m], c0:ncols], V, pt[0:nk, c0:ncols], start=first, stop=last),
                     reads=[key], writes=[("ps", ob[m])])
                P.op("pe", MM(ps[:, zb[m], c0:ncols], onesb[0:nk, :], pt[0:nk, c0:ncols], start=first, stop=last),
                     reads=[key], writes=[("ps", zb[m])])

        nt = len(ktiles)
        for ti, kt in enumerate(ktiles):
            nk, c0, m0 = kt["nk"], kt["c0"], kt["m0"]
            items = []
            for m in range(2):
                sb = main_bank()
                nnear = max(0, min(ncols - c0, 256 - m0))
                grp = [MM(ps[0:nk, sb, c0:ncols], kt["kT"][m], qsrc(m)[:, c0:ncols], start=True, stop=(nnear == 0))]
                if nnear > 0:
                    for hl in range(2):
                        grp.append(MM(ps[0:nk, sb, c0:c0 + nnear], identb[0:nk, 0:nk],
                                      Ehl[0:nk, hl, h, m0:m0 + nnear], start=False, stop=(hl == 1)))
                P.op("pe", grp, reads=["qk"], writes=[("ps", sb)])
                s = rslot("pt", 4)
                pt = view(R4b + 1024 * s, BF16, [512])
                key = ("pt", s)
                if kt.get("mask", 0):
                    assert m0 == 0 and nnear >= 64
                P.op("act", ACT(pt[0:nk, c0:ncols], ps[0:nk, sb, c0:ncols], AF.Exp, bias=c15[0:nk, h:h + 1],
                                scale=SCALE), reads=[("ps", sb)], writes=[key])
                items.append((m, pt, nk, c0, kt["V"], key))
            if prev[0] is not None:
                av(prev[0], prev[1] == 0, False)
            prev = [items, ti]
            tick()
        av(prev[0], prev[1] == 0, True)

    def attn_finish(ncols, ob, zb, dst_fn, silu_fn, keyw):
        T = R4b + 6144
        r1 = view(T, F32, [512])
        r2 = view(T + 2048, F32, [512])
        oc1 = view(T + 4096, F32, [512])
        oc2 = view(T + 6144, F32, [512])
        N = slice(0, ncols)
        o, sq = r1, r2
        P.op("act", ACT(r1[:, N], ps[:, zb[0], N], AF.Ln), reads=[("ps", zb[0])], writes=["r1"])
        P.op("act", ACT(r2[:, N], ps[:, zb[1], N], AF.Ln), reads=[("ps", zb[1])], writes=["r2"])
        P.op("dve", CP(oc1[:, N], ps[:, ob[0], N]), reads=[("ps", ob[0])], writes=["oc1"])
        P.op("dve", CP(oc2[:, N], ps[:, ob[1], N]), reads=[("ps", ob[1])], writes=["oc2"])

        def stB():
            P.op("act", ACT(r1[:, N], r1[:, N], AF.Exp, scale=-1.0), reads=["r1"], writes=["r1"])
            P.op("act", ACT(r2[:, N], r2[:, N], AF.Exp, scale=-1.0), reads=["r2"], writes=["r2"])
            P.op("dve", TT(oc1[:, N], oc1[:, N], r1[:, N], ALU.mult), reads=["oc1", "r1"], writes=["oc1"])
            P.op("dve", TT(oc2[:, N], oc2[:, N], r2[:, N], ALU.mult), reads=["oc2", "r2"], writes=["oc2"])
            P.op("dve", STT(o[:, N], oc2[:, N], neglam[:, 0:1], oc1[:, N], ALU.mult, ALU.add),
                 reads=["oc1", "oc2", "r1"], writes=["r1"])
            defer(0, stB1)

        def stB1():
            P.op("act", ACT(sq[:, N], o[:, N], AF.Square), reads=["r1", "r2"], writes=["r2"])
            defer(0, stB2)

        def stB2():
            mbk = main_bank()
            P.op("pe", MM(ps[:, mbk, N], ones_d, sq[:, N]), reads=["r2"], writes=[("ps", mbk)])
            defer(0, lambda: stC(mbk))

        def stC(mbk):
            P.op("act", ACT(sq[:, N], ps[:, mbk, N], AF.Ln, bias=EPS, scale=1.0), reads=[("ps", mbk)],
                 writes=["r2"])
            P.op("act", ACT(sq[:, N], sq[:, N], AF.Exp, scale=-0.5), reads=["r2"], writes=["r2"])
            P.op("dve", STT(o[:, N], o[:, N], subg[:, 0:1], sq[:, N], ALU.mult, ALU.mult), reads=["r1", "r2"],
                 writes=["r1"])
            P.op("dve", TT(dst_fn(), o_view(o, ncols, dst_fn), silu_fn(), ALU.mult), reads=["r1"], writes=[keyw])
        defer(0, stB)

    def o_view(o, ncols, dst_fn):
        d = dst_fn()
        if len(d.shape) == 2:
            return o[:, 0:ncols]
        return o[:, 0:ncols].rearrange("p (h q) -> p h q", h=d.shape[1])

    def phaseA2(sg):
        hf = sg["hf"]
        for h in range(H):
            pb = 64 * (h % 2)
            blk = h // 2
            for qg in range(2):
                q0 = 512 * qg
                tq0 = 1024 * hf + q0
                ktl = [dict(kT=[kT_a[64 * m:64 * m + 64, h, 1024:1040] for m in range(2)], V=Vt_a[0:16, 8, h, :],
                            nk=16, c0=0, m0=16 + tq0)]
                for kt in range(8 * hf + 4 * qg + 4):
                    c0 = max(0, 128 * kt - tq0)
                    m0 = tq0 + c0 - 128 * kt
                    if kt < 8:
                        kTs = [kT_a[64 * m:64 * m + 64, h, 128 * kt:128 * kt + 128] for m in range(2)]
                        V = Vt_a[:, kt, h, :]
                    else:
                        kTs = [kT_b[64 * m:64 * m + 64, h, 128 * (kt - 8):128 * (kt - 8) + 128] for m in range(2)]
                        V = Vt_b[:, kt - 8, h, :]
                    ktl.append(dict(kT=kTs, V=V, nk=128, c0=c0, m0=m0, mask=(64 if 128 * kt >= tq0 else 0)))
                far = [t for t in ktl[1:] if t["c0"] == 0 and not t["mask"]]
                dg = [t for t in ktl[1:] if not (t["c0"] == 0 and not t["mask"])]
                order = [ktl[0]]
                if far:
                    step = max(1, len(far) // (len(dg) + 1))
                    fi = 0
                    for dt_ in dg:
                        order += far[fi:fi + step]
                        fi += step
                        order.append(dt_)
                    order += far[fi:]
                else:
                    order += dg
                assert len(order) == len(ktl)
                attn_job(h, lambda m, h=h, q0=q0: qT[64 * m:64 * m + 64, h, q0:q0 + 512], 512, order,
                         (4, 5), (6, 7))
                dst = lambda h=h, q0=q0: attnB[:, h, q0:q0 + 512]
                attn_finish(512, (4, 5), (6, 7), dst, dst, ("attnB", h, qg))

    def phaseA2s(sg):
        qs = view(SAMP + 4096, BF16, [8, 32])
        ks = view(SAMP + 4608, BF16, [8, 32])
        vns = view(SAMP, BF16, [2, 8, 128])
        for sq_ in range(2):
            for h in range(H):
                s = rslot("kc", 2)
                kcb = view(TMP + 2048 * s, BF16, [8, 128])
                vcb = view(TMP + 4096 + 2048 * s, BF16, [8, 128])
                P.dma("pool", kcb, cache_k[sq_, h].rearrange("(t p) f -> p t f", p=128), "kc%d" % s,
                      writes=[("kcb", s)])
                P.dma("pool", vcb, cache_v[sq_, h].rearrange("(t p) f -> p t f", p=128), "vc%d" % s,
                      writes=[("vcb", s)])
                kcT = view(TMP + 8192 + 2048 * s, BF16, [1024])
                for half in range(2):
                    bk = main_bank()
                    P.op("pe", [MM(ps[:, bk, 128 * k:128 * k + 128], kcb[:, 4 * half + k, :], identb)
                                for k in range(4)], reads=[("kcb", s)], writes=[("ps", bk)])
                    P.op("dve", CP(kcT[:, 512 * half:512 * half + 512], ps[:, bk, :]), reads=[("ps", bk)],
                         writes=[("kcT", s, half)])
                items = []
                for m in range(2):
                    sb = main_bank()
                    qap = qT[64 * m:64 * m + 64, h, 1040 + 16 * sq_:1056 + 16 * sq_]
                    grp = [MM(ps[:, sb, 16 * t:16 * t + 16], kcT[64 * m:64 * m + 64, 128 * t:128 * t + 128], qap,
                              start=True, stop=(t < 7)) for t in range(8)]
                    for hl in range(2):
                        grp.append(MM(ps[:, sb, 112:128], identb, Ehl[:, hl, h, 128:144], start=False, stop=(hl == 1)))
                    grp.append(MM(ps[0:16, sb, 128:144], ksamp[64 * m:64 * m + 64, h, 16 * sq_:16 * sq_ + 16], qap,
                                  start=True, stop=False))
                    for hl in range(2):
                        grp.append(MM(ps[0:16, sb, 128:144], identb[0:16, 0:16], Ehl[0:16, hl, h, 0:16], start=False,
                                      stop=(hl == 1)))
                    P.op("pe", grp, reads=[("kcT", s, 0), ("kcT", s, 1)], writes=[("ps", sb)])
                    sp_ = rslot("pts", 4)
                    pt = view(R4b + 1024 * sp_, BF16, [512])
                    key = ("pt", sp_)
                    P.op("act", ACT(pt[:, 0:128], ps[:, sb, 0:128], AF.Exp, bias=c15[:, h:h + 1], scale=SCALE),
                         reads=[("ps", sb)], writes=[key])
                    P.op("act", ACT(pt[0:16, 128:144], ps[0:16, sb, 128:144], AF.Exp, bias=c15[0:16, h:h + 1],
                                    scale=SCALE), reads=[("ps", sb)], writes=[key])
                    items.append((m, pt, key))
                for (m, pt, key) in items:
                    oc = (m * 8 + h) * 16
                    g1 = [MM(ps[:, 4, oc:oc + 16], vcb[:, t, :], pt[:, 16 * t:16 * t + 16], start=(t == 0), stop=False)
                          for t in range(8)]
                    g1.append(MM(ps[:, 4, oc:oc + 16], vns[0:16, sq_, h, :], pt[0:16, 128:144], start=False,
                                 stop=True))
                    g2 = [MM(ps[:, 5, oc:oc + 16], onesb, pt[:, 16 * t:16 * t + 16], start=(t == 0), stop=False)
                          for t in range(8)]
                    g2.append(MM(ps[:, 5, oc:oc + 16], onesb[0:16, :], pt[0:16, 128:144], start=False, stop=True))
                    P.op("pe", g1 + g2, reads=[key, ("vcb", s), "vns"], writes=[("ps", 4), ("ps", 5)])
            T = R4b + 6144
            r1 = view(T, F32, [512])
            o = view(T + 4096, F32, [512])
            sq = view(T + 6144, F32, [512])
            P.op("act", ACT(r1[:, 0:256], ps[:, 5, 0:256], AF.Ln), reads=[("ps", 5)], writes=["r1"])
            P.op("act", ACT(r1[:, 0:256], r1[:, 0:256], AF.Exp, scale=-1.0), reads=["r1"], writes=["r1"])
            P.op("dve", TT(r1[:, 0:256], ps[:, 4, 0:256], r1[:, 0:256], ALU.mult), reads=[("ps", 4), "r1"],
                 writes=["r1"])
            P.op("dve", STT(o[:, 0:128], r1[:, 128:256], neglam[:, 0:1], r1[:, 0:128], ALU.mult, ALU.add),
                 reads=["r1"], writes=["o"])
            P.op("act", ACT(sq[:, 0:128], o[:, 0:128], AF.Square), reads=["o"], writes=["sq"])
            mbk = main_bank()
            P.op("pe", MM(ps[:, mbk, 0:128], ones_d, sq[:, 0:128]), reads=["sq"], writes=[("ps", mbk)])
            P.op("act", ACT(sq[:, 0:128], ps[:, mbk, 0:128], AF.Ln, bias=EPS, scale=1.0), reads=[("ps", mbk)],
                 writes=["sq"])
            P.op("act", ACT(sq[:, 0:128], sq[:, 0:128], AF.Exp, scale=-0.5), reads=["sq"], writes=["sq"])
            P.op("dve", STT(o[:, 0:128], o[:, 0:128], subg[:, 0:1], sq[:, 0:128], ALU.mult, ALU.mult),
                 reads=["o", "sq"], writes=["o"])
            dst = attnB[:, :, 1040 + 16 * sq_:1056 + 16 * sq_]
            P.op("dve", TT(dst, o[:, 0:128].rearrange("p (h q) -> p h q", h=8), dst, ALU.mult), reads=["o"],
                 writes=[("attnBs", sq_)])

    ksamp = view(SAMP + 5120, BF16, [8, 32])

    def phaseB(sg):
        b, hf = sg["b"], sg["hf"]
        samp = sg["samp"]
        ncol = 1024 + sg["next"]
        atmp = view(R4, F32, [NCMAX])
        sgt = [view(R4 + 4352 + 2048 * i, F32, [512]) for i in range(2)]
        gl32p = view(R4 + 8448, F32, [8, 30])
        gl32s = view(R4 + 8448 + 960, F32, [8, 32])
        gs = view(R4 + 8448 + 960 + 1024, BF16, [8, 2, 46])
        stt_ = view(TMP + 2048, F32, [2048])
        cv = view(R4b, F32, [8, 512])
        if hf == 0:
            P.op("dve", MS(glu[:, :, 0:14], 0.0), writes=["gluh"])
            if not sg["next"]:
                P.op("dve", CP(glu[:, :, 14:30], gm), writes=["glum_all"])
        else:
            P.op("dve", CP(glu[:, :, 0:30], gh), writes=["gluh"])
        if samp:
            for sq_ in range(2):
                P.dma("sp", stt_[0:30, 1024 * sq_:1024 * sq_ + 1024], state_conv[sq_], "stt%d" % sq_, writes=[("stt", sq_)])
                P.dma("sp", conv_sample[sq_, 0:14, :], state_conv[sq_, 16:30, :], "cst")
                bk = main_bank()
                P.op("pe", [TR(ps[:, bk, 32 * g:32 * g + 30], stt_[0:30, 1024 * sq_ + 128 * g:1024 * sq_ + 128 * g + 128],
                               ident[0:30, 0:30]) for g in range(8)], reads=[("stt", sq_)], writes=[("ps", bk)])
                P.op("dve", CP(gs[:, :, sq_, 0:30], ps[:, bk, 0:256].rearrange("p (g j) -> p g j", g=8)[:, :, 0:30]),
                     reads=[("ps", bk)], writes=[("gs", sq_)])
        for (typ, g, col) in B_blocks:
            wap, wkey = next_w()
            for ci, (c0, n) in enumerate(coltiles(sg)):
                bk = inproj_group(wap, wkey, c0, n)
                if typ == "a":
                    P.op("act", ACT(atmp[:, c0:c0 + n], ps[:, bk, 0:n], AF.Copy), reads=[("ps", bk)],
                         writes=[("atmp", ci)])
                elif typ == "g":
                    s = rslot("sg", 2)
                    sg_ = sgt[s]
                    P.op("act", ACT(sg_[:, 0:n], ps[:, bk, 0:n], AF.Sigmoid), reads=[("ps", bk)], writes=[("sg", s)])
                    if ci < 2:
                        P.op("dve", TT(glu[:, g, 30 + c0:30 + c0 + n], atmp[:, c0:c0 + n], sg_[:, 0:n], ALU.mult),
                             reads=[("atmp", ci), ("sg", s)], writes=[("glu", g, ci)])
                        if ci == 1 and hf == 1:
                            P.op("dve", TT(gl32p[:, g, :], atmp[:, 994:1024], sg_[:, 482:512], ALU.mult),
                                 reads=[("atmp", ci), ("sg", s)], writes=[("gl32p", g)])
                        if ci == 1 and hf == 0:
                            P.op("dve", CP(gh[:, g, :], glu[:, g, 1024:1054]), reads=[("glu", g, ci)],
                                 writes=[("gh", g)])
                    else:
                        if hf == 0:
                            P.op("dve", TT(glu[:, g, 14:30], atmp[:, 1024:1040], sg_[:, 0:16], ALU.mult),
                                 reads=[("atmp", ci), ("sg", s)], writes=[("glum", g)])
                            P.op("dve", CP(gm[:, g, :], glu[:, g, 14:30]), reads=[("glum", g)], writes=[("gm", g)])
                        if samp:
                            P.op("dve", TT(gl32s[:, g, :], atmp[:, 1040:1072], sg_[:, 16:48], ALU.mult),
                                 reads=[("atmp", ci), ("sg", s)], writes=[("gl32s", g)])
                            P.op("dve", CP(gs[:, g, :, 30:46], gl32s[:, g, :].rearrange("p (s t) -> p s t", s=2)),
                                 reads=[("gl32s", g)], writes=[("gsn", g)])
                else:
                    P.op("act", ACT(convB[:, g, c0:c0 + n], ps[:, bk, 0:n], AF.Silu), reads=[("ps", bk)],
                         writes=[("convB", g, ci)])
        flush()
        P.barrier()
        if hf == 1:
            ost = view(TMP + 2048, F32, [1024])
            for half in range(2):
                bk = main_bank()
                P.op("pe", [TR(ps[0:30, bk, 128 * k:128 * k + 128], gl32p[:, 4 * half + k, :], ident) for k in range(4)],
                     reads=[("gl32p", 4 * half + k) for k in range(4)], writes=[("ps", bk)])
                P.op("dve", CP(ost[0:30, 512 * half:512 * half + 512], ps[0:30, bk, :]), reads=[("ps", bk)],
                     writes=[("ost", half)])
            P.dma("sp", conv_prompt[b], ost[0:30, :], "ost", reads=[("ost", 0), ("ost", 1)], writes=[("ost", 0), ("ost", 1)])
        if samp:
            ost2 = view(TMP + 6144, F32, [1024])
            for sq_ in range(2):
                for half in range(2):
                    bk = main_bank()
                    P.op("pe", [TR(ps[0:16, bk, 128 * k:128 * k + 128], gl32s[:, 4 * half + k, 16 * sq_:16 * sq_ + 16],
                                   ident) for k in range(4)],
                         reads=[("gl32s", 4 * half + k) for k in range(4)], writes=[("ps", bk)])
                    P.op("dve", CP(ost2[0:16, 512 * half:512 * half + 512], ps[0:16, bk, :]), reads=[("ps", bk)],
                         writes=[("ost2", half)])
                P.dma("sp", conv_sample[sq_, 14:30, :], ost2[0:16, :], "ost2",
                      reads=[("ost2", 0), ("ost2", 1)], writes=[("ost2", 0), ("ost2", 1)])
        ctl = [(0, 512, False), (512, 512, False)]
        if samp:
            ctl.append((1040, 32, True))
        T2 = R4
        dgs = [view(TMP + 256 * i, BF16, [128]) for i in range(8)]
        cvs = [view(R4b + 8192 * i, BF16, [8, 512]) for i in range(2)]
        mean = view(T2 + 4096, F32, [512])
        rstd = view(T2 + 6144, F32, [512])
        tnm = [view(T2 + 12032 + 2048 * i, F32, [512]) for i in range(2)]

        def conv_ct(idx, after_group=None):
            c0, n, is_s = ctl[idx]
            cv = cvs[idx % 2]
            for g in range(8):
                if after_group is not None and g > 0:
                    after_group(g - 1)
                bk = main_bank()
                bk2 = main_bank() if is_s else None
                for j in range(CW):
                    s = rslot("dg", 8)
                    P.op("dve", TS(dgs[s], identb, cw[:, g, j:j + 1], ALU.mult), writes=[("dg", s)])
                    if not is_s:
                        P.op("pe", MM(ps[:, bk, 0:n], dgs[s], glu[:, g, c0 + j:c0 + j + n], start=(j == 0),
                                      stop=(j == CW - 1)), reads=[("dg", s)], writes=[("ps", bk)])
                    else:
                        P.op("pe", [MM(ps[:, (bk, bk2)[q_], 0:16], dgs[s], gs[:, g, q_, j:j + 16], start=(j == 0),
                                       stop=(j == CW - 1)) for q_ in range(2)],
                             reads=[("dg", s)], writes=[("ps", bk), ("ps", bk2)])
                if not is_s:
                    P.op("act", ACT(cv[:, g, 0:n], ps[:, bk, 0:n], AF.Identity, bias=cb[:, g:g + 1], scale=1.0),
                         reads=[("ps", bk)], writes=[("cv", idx % 2, g)])
                else:
                    for q_ in range(2):
                        P.op("act", ACT(cv[:, g, 16 * q_:16 * q_ + 16], ps[:, (bk, bk2)[q_], 0:16], AF.Identity,
                                        bias=cb[:, g:g + 1], scale=1.0),
                             reads=[("ps", (bk, bk2)[q_])], writes=[("cv", idx % 2, g)])
            if after_group is not None:
                after_group(7)

        def stats_ct(idx):
            c0, n, is_s = ctl[idx]
            cv = cvs[idx % 2]
            for g in range(8):
                s = rslot("sqf", 2)
                sqf = view(T2 + 1024 * s, BF16, [512])
                P.op("act", ACT(sqf[:, 0:n], cv[:, g, 0:n], AF.Square), reads=[("cv", idx % 2, g)], writes=[("sqf", s)])
                P.op("pe", MM(ps[:, 4, 0:n], ones_lnb, cv[:, g, 0:n], start=(g == 0), stop=(g == 7)),
                     reads=[("cv", idx % 2, g)], writes=[("ps", 4)])
                P.op("pe", MM(ps[:, 5, 0:n], ones_lnb, sqf[:, 0:n], start=(g == 0), stop=(g == 7)),
                     reads=[("sqf", s)], writes=[("ps", 5)])

        def norm_a(idx):
            c0, n, is_s = ctl[idx]
            P.op("dve", CP(mean[:, 0:n], ps[:, 4, 0:n]), reads=[("ps", 4)], writes=["mean"])
            P.op("dve", TT(rstd[:, 0:n], mean[:, 0:n], mean[:, 0:n], ALU.mult), reads=["mean"], writes=["rstd"])
            P.op("dve", TT(rstd[:, 0:n], ps[:, 5, 0:n], rstd[:, 0:n], ALU.subtract), reads=[("ps", 5), "rstd"],
                 writes=["rstd"])
            P.op("act", ACT(rstd[:, 0:n], rstd[:, 0:n], AF.Ln, bias=EPS, scale=1.0), reads=["rstd"], writes=["rstd"])
            P.op("act", ACT(rstd[:, 0:n], rstd[:, 0:n], AF.Exp, scale=-0.5), reads=["rstd"], writes=["rstd"])

        def norm_b(idx, g):
            c0, n, is_s = ctl[idx]
            cv = cvs[idx % 2]
            if True:
                s = rslot("tnm", 2)
                t = tnm[s]
                P.op("pool", TT(t[:, 0:n], cv[:, g, 0:n], mean[:, 0:n], ALU.subtract),
                     reads=[("cv", idx % 2, g), "mean"], writes=[("tnm", s)])
                P.op("pool", TT(t[:, 0:n], t[:, 0:n], rstd[:, 0:n], ALU.mult), reads=[("tnm", s), "rstd"],
                     writes=[("tnm", s)])
                P.op("act", ACT(t[:, 0:n], t[:, 0:n], AF.Silu, bias=lnb[:, g:g + 1], scale=lng[:, g:g + 1]),
                     reads=[("tnm", s)], writes=[("tnm", s)])
                P.op("pool", TT(convB[:, g, c0:c0 + n], t[:, 0:n], convB[:, g, c0:c0 + n], ALU.mult),
                     reads=[("tnm", s)], writes=[("convBo", g, idx)])

        for idx in range(len(ctl)):
            if idx > 0:
                conv_ct(idx, after_group=lambda g, i=idx - 1: norm_b(i, g))
            else:
                conv_ct(idx)
            stats_ct(idx)
            norm_a(idx)
        early_gates(sg)
        for g in range(8):
            norm_b(len(ctl) - 1, g)

    def gate_blocks(sg):
        ctl = [(0, 512), (512, 512)]
        if sg["samp"]:
            ctl.append((1040, 32))
        sgc = view(R3, F32, [NCMAX])
        sga = view(R3 + 4352, F32, [NCMAX])
        wgc, kgc = next_w()
        for (c0, n) in ctl:
            b1 = inproj_group(wgc, kgc, c0, n)
            P.op("act", ACT(sgc[:, c0:c0 + n], ps[:, b1, 0:n], AF.Sigmoid), reads=[("ps", b1)],
                 writes=[("sgc", c0)])
        wga, kga = next_w()
        for (c0, n) in ctl:
            b2 = inproj_group(wga, kga, c0, n)
            P.op("act", ACT(sga[:, c0:c0 + n], ps[:, b2, 0:n], AF.Sigmoid), reads=[("ps", b2)],
                 writes=[("sga", c0)])

    def early_gates(sg):
        if EARLY_GATES:
            gate_blocks(sg)

    def phaseC(sg):
        ctl = [(0, 512), (512, 512)]
        if sg["samp"]:
            ctl.append((1040, 32))
        sgc = view(R3, F32, [NCMAX])
        sga = view(R3 + 4352, F32, [NCMAX])
        m1s = [view(R3 + 8704 + 2048 * i, F32, [512]) for i in range(2)]
        m2s = [view(R3 + 12800 + 2048 * i, F32, [512]) for i in range(2)]
        for f in range(16):
            if not (f == 0 and EARLY_GATES):
                gate_blocks(sg)
            wbo, kbo = next_w()
            for (c0, n) in ctl:
                s = rslot("pc", 2)
                m1, m2 = m1s[s], m2s[s]
                b3 = inproj_group(wbo, kbo, c0, n, nk=8, rhs=lambda kc: convB[:, kc, c0:c0 + n], koff=0)
                P.op("dve", TT(m1[:, 0:n], ps[:, b3, 0:n], sgc[:, c0:c0 + n], ALU.mult),
                     reads=[("ps", b3), ("sgc", c0)], writes=[("m1", s)])
                b4 = inproj_group(wbo, kbo, c0, n, nk=8, rhs=lambda kc: attnB[:, kc, c0:c0 + n], koff=8)
                P.op("dve", TT(m2[:, 0:n], ps[:, b4, 0:n], sga[:, c0:c0 + n], ALU.mult),
                     reads=[("ps", b4), ("sga", c0)], writes=[("m2", s)])
                P.op("dve", TT(merged[:, f, c0:c0 + n], m1[:, 0:n], m2[:, 0:n], ALU.add), reads=[("m1", s), ("m2", s)],
                     writes=[("merged", f, c0)])

    def phaseD(sg):
        b, hf = sg["b"], sg["hf"]
        wstate["wout_ok"] = sg["si"]
        tl = [(i, 128) for i in range(8)]
        if sg["samp"]:
            tl.append((8, 32))
        xr = [view(R5 + 2048 * i, F32, [512]) for i in range(3)]
        ys = [view(R5 + 6144 + 2048 * i, F32, [512]) for i in range(3)]

        def xsrc(ti, cg):
            if ti < 8:
                t0 = 1024 * hf + 128 * ti
                return [(slice(0, 128), x_prompt[b, t0:t0 + 128, 512 * cg:512 * cg + 512])]
            return [(slice(16 * q_, 16 * q_ + 16), x_sample[q_, :, 512 * cg:512 * cg + 512]) for q_ in range(2)]

        def ydst(ti, cg):
            if ti < 8:
                t0 = 1024 * hf + 128 * ti
                return [(slice(0, 128), y_prompt[b, t0:t0 + 128, 512 * cg:512 * cg + 512])]
            return [(slice(16 * q_, 16 * q_ + 16), y_sample[q_, :, 512 * cg:512 * cg + 512]) for q_ in range(2)]

        jobs = [(cg, ti, n) for cg in range(4) for (ti, n) in tl]

        def load(idx):
            cg, ti, n = jobs[idx]
            s = idx % 3
            for (sl, src) in xsrc(ti, cg):
                P.dma("sp", xr[s][sl, :], src, "xr%d" % s, writes=[("xr", s)])
        load(0)
        wcur = None
        nxt = []
        if OVERLAP0 and sg["si"] + 1 < len(segs) and not stop_here(sg):
            nxt = phase0_steps(segs[sg["si"] + 1], overlapped=True)
        for idx, (cg, ti, n) in enumerate(jobs):
            if nxt and idx % 2 == 1:
                nxt.pop(0)()
            if idx + 1 < len(jobs):
                load(idx + 1)
            if ti == 0:
                wcur = next_w()
            wap, wkey = wcur
            s = idx % 3
            c0 = 128 * ti if ti < 8 else 1040
            bk = main_bank()
            P.op("pe", [MM(ps[0:n, bk, :], merged[:, kc, c0:c0 + n], wap[:, kc, :], start=(kc == 0), stop=(kc == 15))
                        for kc in range(16)], reads=[wkey, "mergedall"], writes=[("ps", bk)])
            P.op("dve", TT(ys[s][0:n, :], ps[0:n, bk, :], xr[s][0:n, :], ALU.add), reads=[("ps", bk), ("xr", s)],
                 writes=[("ys", s)])
            for (sl, dst) in ydst(ti, cg):
                P.dma("sp", dst, ys[s][sl, :], "ys%d" % s, reads=[("ys", s)], writes=[("ys", s)])
        while nxt:
            nxt.pop(0)()

    stop = os.environ.get("KSTOP", "")
    phases = [("0", phase0), ("A1", phaseA1), ("A2", phaseA2), ("A2s", phaseA2s), ("B", phaseB), ("C", phaseC),
              ("D", phaseD)]
    done = False
    for sg in segs:
        for (pn, fn) in phases:
            if pn == "A2s" and not sg["samp"]:
                continue
            fn(sg)
            flush()
            P.barrier()
            if stop == "%d:%s" % (sg["si"], pn):
                done = True
                break
        if done:
            break
    return nc, P


_CACHE = {}


def _get_program():
    if "nc" not in _CACHE:
        from contextlib import ExitStack
        stack = ExitStack()
        nc, P = build_program(stack)
        block = stack.enter_context(nc.Block())
        P.emit(block)
        stack.close()
        _CACHE["nc"] = nc
    return _CACHE["nc"]


def kernel(x_prompt, x_sample, cache_k, cache_v, state_conv, meta_tokens, rel_bias, norm_gain, w_in,
           q_norm_gain, k_norm_gain, lambda_qk, subln_gain, conv_w, conv_b, conv_ln_gain, conv_ln_bias,
           w_branch_out, w_out):
    f = lambda a: np.ascontiguousarray(np.asarray(a, dtype=np.float32))
    nc = _get_program()
    oh = np.zeros((32, 383), np.float32)
    for s in range(383):
        oh[_bucket(s - 255), s] = 1.0
    shared = {
        "meta_tokens": f(meta_tokens), "rel_bias": f(rel_bias), "norm_gain": f(norm_gain).reshape(1, D),
        "w_in": f(w_in).reshape(D, DIN), "q_norm_gain": f(q_norm_gain).reshape(H, 64),
        "k_norm_gain": f(k_norm_gain).reshape(H, 64), "lambda_qk": f(lambda_qk).reshape(1, 256),
        "subln_gain": f(subln_gain).reshape(128, 1), "conv_w": f(conv_w).reshape(CW, 1024),
        "conv_b": f(conv_b).reshape(1024), "conv_ln_gain": f(conv_ln_gain).reshape(1024),
        "conv_ln_bias": f(conv_ln_bias).reshape(1024), "w_branch_out": f(w_branch_out).reshape(2048, D),
        "w_out": f(w_out).reshape(D, D), "c_ident": np.eye(128, dtype=np.float32), "c_oh": oh,
    }
    xp, xs = f(x_prompt), f(x_sample)
    ck, cvv, sc = f(cache_k)[0], f(cache_v)[0], f(state_conv)[0]
    in_maps = []
    import os
    ncores = int(os.environ.get('KCORES', NCORES))
    for c in range(ncores):
        m = dict(shared)
        m["x_prompt"] = xp[2 * c:2 * c + 2]
        m["x_sample"] = xs[2 * c:2 * c + 2]
        m["cache_k"] = ck[2 * c:2 * c + 2]
        m["cache_v"] = cvv[2 * c:2 * c + 2]
        m["state_conv"] = sc[2 * c:2 * c + 2]
        in_maps.append(m)
    res = run_bass_kernel_spmd(nc, in_maps, core_ids=list(range(ncores)))
    R = res.results
    cat = lambda k: np.concatenate([np.asarray(r[k], dtype=np.float32) for r in R], axis=0)
    return (cat("y_prompt"), cat("y_sample"), cat("k_prompt")[None], cat("v_prompt")[None], cat("conv_prompt")[None],
            cat("k_sample")[None], cat("v_sample")[None], cat("conv_sample")[None])
```

```python
import math
import os
import numpy as np
import concourse.bass as bass
import concourse.mybir as mybir
from concourse.bass_utils import run_bass_kernel_spmd

F32 = mybir.dt.float32
BF16 = mybir.dt.bfloat16
U8 = mybir.dt.uint8
AF = mybir.ActivationFunctionType
ALU = mybir.AluOpType
AX = mybir.AxisListType

NCORES = 8
D = 2048
SEQ = 2048
NMETA = 16
L = SEQ + NMETA
DIN = 11264
H = 8
PAST = 1024
TS_ = 16
CW = 31
EPS = 1e-6
SCALE = 0.125
LAM_INIT = 0.8 - 0.6 * math.exp(0.0)
NCMAX = 1072
LAYOUT = {}


def _bucket(r):
    n = abs(int(r))
    nf = np.float32(max(n, 1))
    large = 8 + int(np.float32(np.log(nf / np.float32(8))) / np.float32(math.log(16.0)) * np.float32(8))
    large = min(large, 15)
    return (16 if r > 0 else 0) + (n if n < 8 else large)


class Prog:
    def __init__(self, nc):
        self.nc = nc
        self.engs = {"pe": nc.tensor, "act": nc.scalar, "dve": nc.vector, "pool": nc.gpsimd, "sp": nc.sync}
        self.q = {k: [] for k in self.engs}
        self.sems = {}
        self.cnt = {}
        self.waited = {k: {} for k in self.engs}
        self.lastw = {}
        self.readers = {}
        self.stack = None

    def sem(self, name):
        if name not in self.sems:
            self.sems[name] = self.stack.enter_context(self.nc.semaphore(name))
            self.cnt[name] = 0
        return name

    def _wait(self, eng, tok):
        name, val, prod = tok
        if prod == "pe" and eng == "pe":
            return
        if self.waited[eng].get(name, 0) >= val:
            return
        self.waited[eng][name] = val
        self.q[eng].append(("wait", name, val))

    def _deps(self, eng, reads, writes):
        toks = []
        for k in reads:
            if k in self.lastw:
                toks.append(self.lastw[k])
        for k in writes:
            if k in self.lastw:
                toks.append(self.lastw[k])
            toks.extend(self.readers.get(k, ()))
        for t in toks:
            self._wait(eng, t)

    def _commit(self, tok, reads, writes):
        for k in reads:
            self.readers.setdefault(k, []).append(tok)
        for k in writes:
            self.lastw[k] = tok
            self.readers[k] = []

    def op(self, eng, fns, reads=(), writes=()):
        if not isinstance(fns, (list, tuple)):
            fns = [fns]
        self._deps(eng, reads, writes)
        name = self.sem("e_" + eng)
        self.cnt[name] += 1
        tok = (name, self.cnt[name], eng)
        for f in fns[:-1]:
            self.q[eng].append(("op", f, None))
        self.q[eng].append(("op", fns[-1], name))
        self._commit(tok, reads, writes)
        return tok

    def dma(self, eng, out, in_, sem, reads=(), writes=(), **kw):
        self._deps(eng, reads, writes)
        name = self.sem(sem)
        self.cnt[name] += 16
        tok = (name, self.cnt[name], "dma")
        self.q[eng].append(("dma", out, in_, name, kw))
        self._commit(tok, reads, writes)
        return tok

    def barrier(self, full=False):
        for eng in self.engs:
            for name, c in self.cnt.items():
                if c > 0 and (full or not name.startswith("w")):
                    self._wait(eng, (name, c, "x"))
        isw = lambda k: (not full) and isinstance(k, tuple) and k[0] in ("w", "wout")
        self.lastw = {k: v for k, v in self.lastw.items() if isw(k)}
        self.readers = {k: v for k, v in self.readers.items() if isw(k)}

    def emit(self, block):
        def run(qname):
            def f(e):
                for it in self.q[qname]:
                    if it[0] == "wait":
                        e.wait_ge(self.sems[it[1]], it[2])
                    elif it[0] == "op":
                        ins = it[1](e)
                        if it[2] is not None:
                            ins.then_inc(self.sems[it[2]], 1)
                    else:
                        e.dma_start(out=it[1], in_=it[2], **it[4]).then_inc(self.sems[it[3]], 16)
            return f
        block.tensor(run("pe"))
        block.scalar(run("act"))
        block.vector(run("dve"))
        block.gpsimd(run("pool"))
        block.sync(run("sp"))


def MM(out, lhsT, rhs, start=True, stop=True):
    return lambda e: e.matmul(out, lhsT=lhsT, rhs=rhs, start=start, stop=stop)


def TR(out, in_, ident):
    return lambda e: e.transpose(out=out, in_=in_, identity=ident)


def ACT(out, in_, func, bias=None, scale=None, accum=None):
    kw = {}
    if bias is not None:
        kw["bias"] = bias
    if scale is not None:
        kw["scale"] = scale
    if accum is not None:
        kw["accum_out"] = accum
    return lambda e: e.activation(out=out, in_=in_, func=func, **kw)


def TS(out, in0, s1, op0, s2=None, op1=None):
    if op1 is None:
        return lambda e: e.tensor_scalar(out=out, in0=in0, scalar1=s1, scalar2=None, op0=op0)
    return lambda e: e.tensor_scalar(out=out, in0=in0, scalar1=s1, scalar2=s2, op0=op0, op1=op1)


def STT(out, in0, scalar, in1, op0, op1):
    return lambda e: e.scalar_tensor_tensor(out=out, in0=in0, scalar=scalar, in1=in1, op0=op0, op1=op1)


def TT(out, in0, in1, op):
    return lambda e: e.tensor_tensor(out=out, in0=in0, in1=in1, op=op)


def CP(out, in_):
    return lambda e: e.tensor_copy(out=out, in_=in_)


def RCP(out, in_):
    return lambda e: e.reciprocal(out=out, in_=in_)


def MS(ap, v):
    return lambda e: e.memset(ap, v)


def build_program(stack):
    nc = bass.Bass("TRN2", target_bir_lowering=False)
    P = Prog(nc)
    P.stack = stack

    def din(name, shape):
        return nc.dram_tensor(name, list(shape), F32, kind="ExternalInput").ap()

    def dout(name, shape):
        return nc.dram_tensor(name, list(shape), F32, kind="ExternalOutput").ap()

    x_prompt = din("x_prompt", [2, SEQ, D])
    x_sample = din("x_sample", [2, TS_, D])
    cache_k = din("cache_k", [2, H, PAST, 128])
    cache_v = din("cache_v", [2, H, PAST, 128])
    state_conv = din("state_conv", [2, 30, 1024])
    meta_tokens = din("meta_tokens", [NMETA, D])
    rel_bias = din("rel_bias", [32, H])
    norm_gain = din("norm_gain", [1, D])
    w_in = din("w_in", [D, DIN])
    q_norm_gain = din("q_norm_gain", [H, 64])
    k_norm_gain = din("k_norm_gain", [H, 64])
    lambda_qk = din("lambda_qk", [1, 256])
    subln_gain = din("subln_gain", [128, 1])
    conv_w = din("conv_w", [CW, 1024])
    conv_b = din("conv_b", [1024])
    conv_ln_gain = din("conv_ln_gain", [1024])
    conv_ln_bias = din("conv_ln_bias", [1024])
    w_bo = din("w_branch_out", [2048, D])
    w_out = din("w_out", [D, D])
    c_ident = din("c_ident", [128, 128])
    c_oh = din("c_oh", [32, 383])

    y_prompt = dout("y_prompt", [2, SEQ, D])
    y_sample = dout("y_sample", [2, TS_, D])
    k_prompt = dout("k_prompt", [2, H, L, 128])
    v_prompt = dout("v_prompt", [2, H, L, 128])
    conv_prompt = dout("conv_prompt", [2, 30, 1024])
    k_sample = dout("k_sample", [2, H, TS_, 128])
    v_sample = dout("v_sample", [2, H, TS_, 128])
    conv_sample = dout("conv_sample", [2, 30, 1024])
    gscr = nc.dram_tensor("gscr", [H, 383], F32, kind="Internal").ap()

    ARENA = 212000
    arena = stack.enter_context(nc.sbuf_tensor("arena", [128, ARENA], U8))
    ps = stack.enter_context(nc.psum_tensor("ps", [128, 8, 512], F32))
    cur = [0]

    def carve(nbytes):
        o = cur[0]
        cur[0] += (nbytes + 31) // 32 * 32
        assert cur[0] <= ARENA, cur[0]
        return o

    def view(off, dt, shape):
        esz = 4 if dt == F32 else 2
        n = int(np.prod(shape))
        v = arena[:, off:off + n * esz].bitcast(dt)
        if len(shape) == 1:
            return v
        names = " ".join("d%d" % i for i in range(len(shape)))
        kw = {"d%d" % i: shape[i] for i in range(len(shape))}
        return v.rearrange("p (%s) -> p %s" % (names, names), **kw)

    def alloc(dt, shape):
        esz = 4 if dt == F32 else 2
        return view(carve(int(np.prod(shape)) * esz), dt, shape)

    R1 = carve(16 * NCMAX * 2)
    R2 = carve(8 * 1040 * 2 + 9 * 8 * 128 * 2)
    R3 = carve(34048)
    R4 = carve(16 * NCMAX * 2)
    R5 = carve(8 * NCMAX * 2)
    hT = view(R1, BF16, [16, NCMAX])
    kT_a = view(R2, BF16, [8, 1040])
    Vt_a = view(R2 + 8 * 1040 * 2, BF16, [9, 8, 128])
    kT_b = view(R3, BF16, [8, 1024])
    Vt_b = view(R3 + 16384, BF16, [8, 8, 128])
    GW = 30 + 1024 + 2
    glu = view(R3, BF16, [8, GW])
    convB = view(R3 + 8 * GW * 2, BF16, [8, NCMAX])
    wout_s = [view(R3 + i * 16384, BF16, [16, 512]) for i in range(2)]
    xt_s = [view(R3 + i * 8192, F32, [2048]) for i in range(2)]
    hb_s = [view(R3 + 16384 + i * 4096, BF16, [2048]) for i in range(2)]
    qT = view(R4, BF16, [8, NCMAX])
    R4b = R4 + 8 * NCMAX * 2
    merged = view(R4, BF16, [16, NCMAX])
    attnB = view(R5, BF16, [8, NCMAX])
    wslot = [alloc(BF16, [16, 128]) for _ in range(3)]
    Ehl = alloc(BF16, [2, 8, 256])
    gain_b = alloc(F32, [2048])
    ident = alloc(F32, [128])
    identb = alloc(BF16, [128])
    blk64 = alloc(BF16, [128])
    onesb = alloc(BF16, [128])
    ones_ln = alloc(F32, [128])
    ones_lnb = alloc(BF16, [128])
    ones_d = alloc(F32, [128])
    gq = alloc(F32, [8])
    gk = alloc(F32, [8])
    subg = alloc(F32, [1])
    cb = alloc(F32, [8])
    lng = alloc(F32, [8])
    lnb = alloc(F32, [8])
    cw = alloc(F32, [8, CW])
    c15 = alloc(F32, [8])
    neglam = alloc(F32, [1])
    gh = alloc(BF16, [8, 30])
    gm = alloc(BF16, [8, 16])
    small = alloc(F32, [16])
    TMP = carve(12288)
    SAMP = carve(5632)
    LAYOUT.update(R1=R1, R2=R2, R3=R3, R4=R4, R5=R5, GW=GW)
    print("arena used", cur[0])

    bank = lambda i: ps[:, i, :]

    wq = []
    wstate = {"emitted": 0, "cur": 0, "nblk": 0, "nwout": 0, "wout_ok": -1}

    def w_in_blk(col):
        if isinstance(col, tuple):
            c = col[1]
            return [w_in[:, c + 512 * t:c + 512 * t + 64].rearrange("(kc p) c -> p kc c", p=128) for t in range(2)]
        return w_in[:, col:col + 128].rearrange("(kc p) c -> p kc c", p=128)

    slot_last = {"blk": [-1, -1, -1], "wout": [-1, -1]}

    def emit_w(icur):
        while wstate["emitted"] < min(icur + 3, len(wq)):
            idx = wstate["emitted"]
            ent = wq[idx]
            if ent["kind"] == "wout":
                if ent["seg"] > wstate["wout_ok"]:
                    return
                s = wstate["nwout"] % 2
                if slot_last["wout"][s] >= icur:
                    return
                slot_last["wout"][s] = idx
                wstate["nwout"] += 1
                ent["ap"] = wout_s[s]
                ent["key"] = ("wout", s)
                P.dma("pool", wout_s[s], ent["src"], "wo%d" % s, writes=[ent["key"]])
            else:
                s = wstate["nblk"] % 3
                if slot_last["blk"][s] >= icur:
                    return
                slot_last["blk"][s] = idx
                wstate["nblk"] += 1
                ent["ap"] = wslot[s]
                ent["key"] = ("w", s)
                if isinstance(ent["src"], list):
                    for t, src in enumerate(ent["src"]):
                        P.dma("pool", wslot[s][:, :, 64 * t:64 * t + 64], src, "w%d" % s, writes=[ent["key"]])
                else:
                    P.dma("pool", wslot[s], ent["src"], "w%d" % s, writes=[ent["key"]])
            wstate["emitted"] += 1

    def next_w():
        i = wstate["cur"]
        emit_w(i)
        assert wstate["emitted"] > i, "weight not emitted"
        wstate["cur"] += 1
        return wq[i]["ap"], wq[i]["key"]

    OVERLAP0 = not os.environ.get('KNO_OVERLAP0')
    EARLY_GATES = not os.environ.get('KNO_EARLY')
    stop = os.environ.get("KSTOP", "")

    def stop_here(sg):
        return stop == "%d:D" % sg["si"]

    segs = []
    for bl in range(2):
        segs.append(dict(si=2 * bl, b=bl, hf=0, next=48 if bl == 0 else 0, samp=(bl == 0)))
        segs.append(dict(si=2 * bl + 1, b=bl, hf=1, next=0, samp=False))

    A1_blocks = []
    for hh_ in range(8):
        A1_blocks += [("k", hh_, ("pair", 1024 + 64 * hh_)), ("q", hh_, ("pair", 64 * hh_))]
    for j in range(8):
        A1_blocks += [("v", j, 2048 + 128 * j), ("z", j, 3072 + 128 * j)]
    B_blocks = []
    for g in range(8):
        B_blocks += [("a", g, 4096 + 128 * g), ("g", g, 5120 + 128 * g)]
    for g in range(8):
        B_blocks += [("zc", g, 6144 + 128 * g)]
    for sg in segs:
        for (_, _, col) in A1_blocks + B_blocks:
            wq.append(dict(kind="blk", src=w_in_blk(col)))
        for f in range(16):
            wq.append(dict(kind="blk", src=w_in_blk(7168 + 128 * f)))
            wq.append(dict(kind="blk", src=w_in_blk(9216 + 128 * f)))
            wq.append(dict(kind="blk", src=w_bo[:, 128 * f:128 * f + 128].rearrange("(kc p) c -> p kc c", p=128)))
        for cg in range(4):
            wq.append(dict(kind="wout", seg=sg["si"],
                           src=w_out[:, 512 * cg:512 * cg + 512].rearrange("(kc p) c -> p kc c", p=128)))

    pend = []

    def defer(delay, fn):
        pend.append([delay, fn])

    def tick():
        todo = [p for p in pend if p[0] <= 0]
        for p in todo:
            pend.remove(p)
        for p in pend:
            p[0] -= 1
        for p in todo:
            p[1]()

    def flush():
        while pend:
            tick()

    mb = [0]

    def main_bank():
        mb[0] = (mb[0] + 1) % 4
        return mb[0]

    ab = [0]

    def aux_bank():
        ab[0] = (ab[0] + 1) % 4
        return 4 + ab[0]

    rot = {}

    def rslot(name, n):
        rot[name] = (rot.get(name, -1) + 1) % n
        return rot[name]

    def bcast_rows(src_ap_tensor, offset, n):
        return bass.AP(tensor=src_ap_tensor, offset=offset, ap=[[0, 128], [1, n]])

    P.dma("sp", ident, c_ident, "cst", writes=["ident"])
    P.dma("sp", gain_b, bcast_rows(norm_gain.tensor, 0, 2048), "cst", writes=["gain_b"])
    P.dma("sp", c15, bcast_rows(rel_bias.tensor, 15 * H, H), "cst", writes=["c15"])
    for two in range(2):
        P.dma("sp", gq[64 * two:64 * two + 64, :], q_norm_gain.rearrange("h d -> d h"), "cst", writes=[("gq", two)],
              allow_slow_non_contiguous=True)
        P.dma("sp", gk[64 * two:64 * two + 64, :], k_norm_gain.rearrange("h d -> d h"), "cst", writes=[("gk", two)],
              allow_slow_non_contiguous=True)
    P.dma("sp", subg, subln_gain, "cst", writes=["subg"])
    P.dma("sp", cb, conv_b.rearrange("(g p) -> p g", p=128), "cst", writes=["cb"], allow_slow_non_contiguous=True)
    P.dma("sp", lng, conv_ln_gain.rearrange("(g p) -> p g", p=128), "cst", writes=["lng"],
          allow_slow_non_contiguous=True)
    P.dma("sp", lnb, conv_ln_bias.rearrange("(g p) -> p g", p=128), "cst", writes=["lnb"],
          allow_slow_non_contiguous=True)
    lq = view(R4, F32, [256])
    P.dma("sp", lq, bcast_rows(lambda_qk.tensor, 0, 256), "cst", writes=["lq"])
    cwr = view(R4 + 1024, F32, [1024])
    P.dma("sp", cwr[0:CW, :], conv_w, "cst", writes=["cwr"])
    tbl = view(R4 + 5120, F32, [8])
    P.dma("sp", tbl[0:32, :], rel_bias, "cst", writes=["tbl"])
    ohs = view(R4 + 5200, F32, [383])
    P.dma("sp", ohs[0:32, :], c_oh, "cst", writes=["ohs"])
    P.barrier()
    P.op("dve", CP(identb, ident), reads=["ident"], writes=["identb"])
    P.op("dve", MS(blk64, 0.0), writes=["blk64"])
    P.op("dve", MS(blk64[0:64, 0:64], 1.0 / 64), writes=["blk64"])
    P.op("dve", MS(blk64[64:128, 64:128], 1.0 / 64), writes=["blk64"])
    P.op("dve", MS(onesb, 1.0), writes=["onesb"])
    P.op("dve", MS(ones_ln, 1.0 / 1024), writes=["ones_ln"])
    P.op("dve", MS(ones_lnb, 1.0 / 1024), writes=["ones_lnb"])
    P.op("dve", MS(ones_d, 1.0 / 128), writes=["ones_d"])
    P.op("dve", MS(gh, 0.0), writes=["gh"])
    P.op("dve", TS(subg, subg, 1.0 - LAM_INIT, ALU.mult), reads=["subg"], writes=["subg"])
    l1 = view(R4 + 8192, F32, [64])
    l2 = view(R4 + 8192 + 256, F32, [64])
    P.op("dve", TT(l1, lq[:, 0:64], lq[:, 64:128], ALU.mult), reads=["lq"], writes=["l1"])
    P.op("dve", TT(l2, lq[:, 128:192], lq[:, 192:256], ALU.mult), reads=["lq"], writes=["l2"])
    P.op("dve", lambda e: e.reduce_sum(out=small[:, 0:1], in_=l1, axis=AX.X), reads=["l1"], writes=["s0"])
    P.op("dve", lambda e: e.reduce_sum(out=small[:, 1:2], in_=l2, axis=AX.X), reads=["l2"], writes=["s1"])
    P.op("act", ACT(small[:, 2:4], small[:, 0:2], AF.Exp), reads=["s0", "s1"], writes=["s2"])
    P.op("dve", TT(small[:, 4:5], small[:, 3:4], small[:, 2:3], ALU.subtract), reads=["s2"], writes=["s4"])
    P.op("dve", TS(neglam, small[:, 4:5], -LAM_INIT, ALU.add), reads=["s4"], writes=["neglam"])
    for g in range(8):
        P.op("pe", TR(ps[:, 4, g * 32:g * 32 + CW], cwr[0:CW, g * 128:(g + 1) * 128], ident[0:CW, 0:CW]),
             reads=["cwr", "ident"], writes=[("ps", 4)])
    P.op("dve", CP(cw, ps[:, 4, 0:256].rearrange("p (g j) -> p g j", g=8)[:, :, 0:CW]), reads=[("ps", 4)],
         writes=["cw"])
    P.op("pe", MM(ps[0:8, 5, 0:383], tbl[0:32, :], ohs[0:32, :]), reads=["tbl", "ohs"], writes=[("ps", 5)])
    gsb = view(R4 + 8192 + 1024, F32, [383])
    P.op("dve", CP(gsb[0:8, :], ps[0:8, 5, 0:383]), reads=[("ps", 5)], writes=["gsb"])
    P.dma("sp", gscr, gsb[0:8, :], "cst", reads=["gsb"], writes=["gscr"])
    P.barrier()
    erev = view(R1, F32, [8, 256])
    for h in range(H):
        P.dma("sp", erev[:, h, :], bass.AP(tensor=gscr.tensor, offset=h * 383, ap=[[1, 128], [1, 256]]), "cst",
              reads=["gscr"], writes=[("erev", h)])
    P.barrier()
    E = view(R5, F32, [8, 256])
    for h in range(H):
        b0 = erev[:, h, :]
        rv = bass.AP(tensor=b0.tensor, offset=b0.offset + 255, ap=[list(b0.ap[0]), [-1, 256]])
        P.op("dve", CP(E[:, h, :], rv), reads=[("erev", h)], writes=["E"])
    P.op("dve", MS(E[64:128, :, 0:64], -30000.0), writes=["E"])
    for h in range(H):
        P.op("dve", TS(E[:, h, :], E[:, h, :], c15[:, h:h + 1], ALU.subtract, 1.0 / SCALE, ALU.mult), reads=["c15"],
             writes=["E"])
    P.op("dve", CP(Ehl[:, 0], E), writes=["E", "Ehl"])
    P.op("dve", TT(E, E, Ehl[:, 0], ALU.subtract), writes=["E"])
    P.op("dve", CP(Ehl[:, 1], E), writes=["E", "Ehl1"])
    P.barrier()

    def coltiles(sg):
        ct = [(0, 512), (512, 512)]
        if sg["next"]:
            ct.append((1024, sg["next"]))
        return ct

    ss_d = alloc(F32, [4])

    def phase0_steps(sg, overlapped=False):
        b = sg["b"]
        tiles = [(i, 128) for i in range(8)]
        if sg["next"]:
            tiles.append((8, sg["next"]))
        steps = []
        for (ti, n) in tiles:
            def step(ti=ti, n=n):
                if overlapped:
                    s = 0
                    xt, hb = view(TMP, F32, [2048]), view(TMP + 8192, BF16, [2048])
                    ss = ss_d
                    kx, semn = "xD", "xtD"
                    dq = "act"
                else:
                    s = rslot("xt", 2)
                    xt, hb = xt_s[s], hb_s[s]
                    ss = view(TMP + 64 * s, F32, [4])
                    kx, semn = "x", "xt%d" % s
                    dq = "sp"
                if ti < 8:
                    t0 = 1024 * sg["hf"] + 128 * ti
                    P.dma(dq, xt, x_prompt[b, t0:t0 + 128, :], semn, writes=[(kx + "t", s)])
                else:
                    P.dma(dq, xt[0:16, :], meta_tokens, semn, writes=[(kx + "t", s)])
                    if sg["samp"]:
                        P.dma(dq, xt[16:32, :], x_sample[0], semn, writes=[(kx + "t", s)])
                        P.dma(dq, xt[32:48, :], x_sample[1], semn, writes=[(kx + "t", s)])
                P.op("dve", MS(ss[0:n, 0:1], 0.0), writes=[(kx + "ss", s)])
                P.op("act", ACT(hb[0:n, :], xt[0:n, :], AF.Square, accum=ss[0:n, 0:1]), reads=[(kx + "t", s)],
                     writes=[(kx + "hb", s), (kx + "ss", s)])
                P.op("act", ACT(ss[0:n, 1:2], ss[0:n, 0:1], AF.Ln, bias=EPS, scale=1.0 / D), reads=[(kx + "ss", s)],
                     writes=[(kx + "ss1", s)])
                P.op("act", ACT(ss[0:n, 2:3], ss[0:n, 1:2], AF.Exp, scale=-0.5), reads=[(kx + "ss1", s)],
                     writes=[(kx + "ss2", s)])
                P.op("dve", STT(hb[0:n, :], xt[0:n, :], ss[0:n, 2:3], gain_b[0:n, :], ALU.mult, ALU.mult),
                     reads=[(kx + "t", s), (kx + "ss2", s)], writes=[(kx + "hb", s)])
            def step_b(ti=ti, n=n):
                if overlapped:
                    s = 0
                    hb = view(TMP + 8192, BF16, [2048])
                    kx = "xD"
                else:
                    s = rot["xt"]
                    hb = hb_s[s]
                    kx = "x"
                c0 = 128 * ti
                for g4 in range(4):
                    bk = main_bank()
                    P.op("pe", [MM(ps[:, bk, 128 * k:128 * k + n], hb[0:n, (4 * g4 + k) * 128:(4 * g4 + k + 1) * 128],
                                   identb[0:n, 0:n]) for k in range(4)],
                         reads=[(kx + "hb", s)], writes=[("ps", bk)])
                    src = ps[:, bk, :].rearrange("p (a c) -> p a c", a=4)[:, :, 0:n]
                    eng = "dve" if g4 % 2 == 0 else "act"
                    fn = CP(hT[:, 4 * g4:4 * g4 + 4, c0:c0 + n], src) if eng == "dve" else \
                        ACT(hT[:, 4 * g4:4 * g4 + 4, c0:c0 + n], src, AF.Copy)
                    P.op(eng, fn, reads=[("ps", bk)], writes=[("hT", ti)])
            steps.append(step)
            steps.append(step_b)
        return steps

    def phase0(sg):
        if sg["si"] > 0 and OVERLAP0:
            return
        for st in phase0_steps(sg):
            st()

    def inproj_group(wap, wkey, c0, n, nk=16, rhs=None, koff=0):
        bk = main_bank()
        rhs = rhs or (lambda kc: hT[:, kc, c0:c0 + n])
        P.op("pe", [MM(ps[:, bk, 0:n], wap[:, koff + kc, :], rhs(kc), start=(kc == 0), stop=(kc == nk - 1))
                    for kc in range(nk)],
             reads=[wkey, "hTall"], writes=[("ps", bk)])
        tick()
        return bk

    def phaseA1(sg):
        b, hf = sg["b"], sg["hf"]
        kst = view(R4b, F32, [9, 128])
        vst = view(R4b + 9 * 2 * 128 * 4, F32, [9, 128])
        pos0 = NMETA + 1024 * hf
        import os
        nb_ = int(os.environ.get('KBLK', 99))
        for bi_, (typ, j, col) in enumerate(A1_blocks):
            if bi_ >= nb_:
                break
            wap, wkey = next_w()
            for ci, (c0, n) in enumerate(coltiles(sg)):
                bk = inproj_group(wap, wkey, c0, n)
                if typ in ("q", "k"):
                    s = rslot("sq", 2)
                    sqb = view(TMP + 1024 * s, BF16, [512])
                    rs = view(TMP + 2048 + 2048 * s, F32, [512])
                    P.op("act", ACT(sqb[:, 0:n], ps[:, bk, 0:n], AF.Square), reads=[("ps", bk)], writes=[("sqb", s)])

                    def st1(typ=typ, j=j, ci=ci, c0=c0, n=n, bk=bk, s=s, sqb=sqb, rs=rs):
                        ax = aux_bank()
                        P.op("pe", MM(ps[:, ax, 0:n], blk64, sqb[:, 0:n]), reads=[("sqb", s)], writes=[("ps", ax)])
                        P.op("act", ACT(rs[:, 0:n], ps[:, ax, 0:n], AF.Ln, bias=EPS, scale=1.0),
                             reads=[("ps", ax)], writes=[("rs", s)])
                        P.op("act", ACT(rs[:, 0:n], rs[:, 0:n], AF.Exp, scale=-0.5), reads=[("rs", s)],
                             writes=[("rs", s)])
                        if typ == "q":
                            P.op("dve", STT(qT[:, j, c0:c0 + n], ps[:, bk, 0:n], gq[:, j:j + 1], rs[:, 0:n],
                                            ALU.mult, ALU.mult),
                                 reads=[("ps", bk), ("rs", s)], writes=[("qT", j, ci)])
                            return
                        s2 = rslot("kn", 2)
                        kn = view(TMP + 6144 + 2048 * s2, F32, [512])
                        P.op("dve", STT(kn[:, 0:n], ps[:, bk, 0:n], gk[:, j:j + 1], rs[:, 0:n],
                                        ALU.mult, ALU.mult),
                             reads=[("ps", bk), ("rs", s)], writes=[("kn", s2)])
                        if ci < 2:
                            dst = (kT_b if hf else kT_a)[:, j, c0:c0 + n]
                            P.op("act", ACT(dst, kn[:, 0:n], AF.Copy), reads=[("kn", s2)], writes=[("kT", j, ci)])
                        else:
                            P.op("act", ACT(kT_a[:, j, 1024:1040], kn[:, 0:16], AF.Copy), reads=[("kn", s2)],
                                 writes=[("kT", j, ci)])
                            if sg["samp"]:
                                P.op("act", ACT(ksamp[:, j, :], kn[:, 16:48], AF.Copy), reads=[("kn", s2)],
                                     writes=[("kTs", j)])

                        def st2():
                            ax2 = aux_bank()
                            if ci < 2:
                                P.op("pe", [TR(ps[:, ax2, 128 * k:128 * k + 128], kn[:, 128 * k:128 * k + 128], ident)
                                            for k in range(4)], reads=[("kn", s2)], writes=[("ps", ax2)])
                                P.op("dve", CP(kst[:, 4 * ci:4 * ci + 4, :],
                                               ps[:, ax2, :].rearrange("p (s d) -> p s d", s=4)),
                                     reads=[("ps", ax2)], writes=[("kst", ci)])
                            else:
                                P.op("pe", TR(ps[0:n, ax2, 0:128], kn[:, 0:n], ident), reads=[("kn", s2)],
                                     writes=[("ps", ax2)])
                                P.op("dve", CP(kst[0:n, 8, :], ps[0:n, ax2, 0:128]),
                                     reads=[("ps", ax2)], writes=[("kst", ci)])
                            if ci == len(coltiles(sg)) - 1:
                                rk = [("kst", c) for c in range(3)]
                                P.dma("sp", k_prompt[b, j, pos0:pos0 + 1024, :].rearrange("(t p) f -> p t f", p=128),
                                      kst[:, 0:8, :], "kst", reads=rk, writes=rk)
                                if sg["next"]:
                                    for b2_ in range(2):
                                        P.dma("sp", k_prompt[b2_, j, 0:16, :], kst[0:16, 8, :], "kst", reads=rk,
                                              writes=rk)
                                if sg["samp"]:
                                    for sq_ in range(2):
                                        P.dma("sp", k_sample[sq_, j, :, :], kst[16 + 16 * sq_:32 + 16 * sq_, 8, :],
                                              "kst", reads=rk, writes=rk)
                        defer(1, st2)
                    defer(1, st1)
                elif typ == "v":
                    s = rslot("vf", 2)
                    vf = view(TMP + 6144 + 2048 * s, F32, [512])
                    P.op("act", ACT(vf[:, 0:n], ps[:, bk, 0:n], AF.Copy), reads=[("ps", bk)], writes=[("kn", s)])

                    def st1(j=j, ci=ci, c0=c0, n=n, s=s, vf=vf):
                        ax = aux_bank()
                        kvm = int(os.environ.get('KV_MODE', 9))
                        if kvm < 1:
                            return
                        if ci < 2:
                            P.op("pe", [TR(ps[:, ax, 128 * k:128 * k + 128], vf[:, 128 * k:128 * k + 128], ident)
                                        for k in range(4)], reads=[("kn", s)], writes=[("ps", ax)])
                            src = ps[:, ax, :].rearrange("p (s d) -> p s d", s=4)
                            if kvm < 2:
                                return
                            P.op("dve", CP(vst[:, 4 * ci:4 * ci + 4, :], src), reads=[("ps", ax)],
                                 writes=[("vst", ci)])
                            if kvm < 3:
                                return
                            Vt = Vt_b if hf else Vt_a
                            P.op("act", ACT(Vt[:, 4 * ci:4 * ci + 4, j, :], vst[:, 4 * ci:4 * ci + 4, :], AF.Copy),
                                 reads=[("vst", ci)], writes=[("Vt", j, ci)])
                        else:
                            if kvm < 4:
                                return
                            P.op("pe", TR(ps[0:n, ax, 0:128], vf[:, 0:n], ident), reads=[("kn", s)],
                                 writes=[("ps", ax)])
                            P.op("dve", CP(vst[0:n, 8, :], ps[0:n, ax, 0:128]), reads=[("ps", ax)],
                                 writes=[("vst", ci)])
                            P.op("act", ACT(Vt_a[0:16, 8, j, :], vst[0:16, 8, :], AF.Copy), reads=[("vst", ci)],
                                 writes=[("Vt", j, ci)])
                            if sg["samp"] and not os.environ.get('KV_NOVNS'):
                                vns = view(SAMP, BF16, [2, 8, 128])
                                for sq_ in range(2):
                                    P.op("pe", TR(ps[0:16, ax, 128 + 128 * sq_:256 + 128 * sq_],
                                                  vf[:, 16 + 16 * sq_:32 + 16 * sq_], ident),
                                         reads=[("kn", s)], writes=[("ps", ax)])
                                P.op("act", ACT(vns[0:16, :, j, :],
                                                ps[0:16, ax, 128:384].rearrange("p (s d) -> p s d", s=2), AF.Copy),
                                     reads=[("ps", ax)], writes=[("vns", j)])
                        if ci == len(coltiles(sg)) - 1 and not os.environ.get('KV_NODMA'):
                            rk = [("vst", c) for c in range(3)]
                            P.dma("sp", v_prompt[b, j, pos0:pos0 + 1024, :].rearrange("(t p) f -> p t f", p=128),
                                  vst[:, 0:8, :], "vst", reads=rk, writes=rk)
                            if sg["next"]:
                                for b2_ in range(2):
                                    P.dma("sp", v_prompt[b2_, j, 0:16, :], vst[0:16, 8, :], "vst", reads=rk, writes=rk)
                            if sg["samp"]:
                                for sq_ in range(2):
                                    P.dma("sp", v_sample[sq_, j, :, :], vst[16 + 16 * sq_:32 + 16 * sq_, 8, :], "vst",
                                          reads=rk, writes=rk)
                    defer(1, st1)
                else:
                    P.op("act", ACT(attnB[:, j, c0:c0 + n], ps[:, bk, 0:n], AF.Silu), reads=[("ps", bk)],
                         writes=[("attnB", j, ci)])
        flush()

    def attn_job(h, qsrc, ncols, ktiles, ob, zb):
        prev = [None]

        def av(items, first, last):
            for (m, pt, nk, c0, V, key) in items:
                P.op("pe", MM(ps[:, ob[m], c0:ncols], V, pt[0:nk, c0:ncols], start=first, stop=last),
                     reads=[key], writes=[("ps", ob[m])])
                P.op("pe", MM(ps[:, zb[m], c0:ncols], onesb[0:nk, :], pt[0:nk, c0:ncols], start=first, stop=last),
                     reads=[key], writes=[("ps", zb[m])])

        nt = len(ktiles)
        for ti, kt in enumerate(ktiles):
            nk, c0, m0 = kt["nk"], kt["c0"], kt["m0"]
            items = []
            for m in range(2):
                sb = main_bank()
                nnear = max(0, min(ncols - c0, 256 - m0))
                grp = [MM(ps[0:nk, sb, c0:ncols], kt["kT"][m], qsrc(m)[:, c0:ncols], start=True, stop=(nnear == 0))]
                if nnear > 0:
                    for hl in range(2):
                        grp.append(MM(ps[0:nk, sb, c0:c0 + nnear], identb[0:nk, 0:nk],
                                      Ehl[0:nk, hl, h, m0:m0 + nnear], start=False, stop=(hl == 1)))
                P.op("pe", grp, reads=["qk"], writes=[("ps", sb)])
                s = rslot("pt", 4)
                pt = view(R4b + 1024 * s, BF16, [512])
                key = ("pt", s)
                if kt.get("mask", 0):
                    assert m0 == 0 and nnear >= 64
                P.op("act", ACT(pt[0:nk, c0:ncols], ps[0:nk, sb, c0:ncols], AF.Exp, bias=c15[0:nk, h:h + 1],
                                scale=SCALE), reads=[("ps", sb)], writes=[key])
                items.append((m, pt, nk, c0, kt["V"], key))
            if prev[0] is not None:
                av(prev[0], prev[1] == 0, False)
            prev = [items, ti]
            tick()
        av(prev[0], prev[1] == 0, True)

    def attn_finish(ncols, ob, zb, dst_fn, silu_fn, keyw):
        T = R4b + 6144
        r1 = view(T, F32, [512])
        r2 = view(T + 2048, F32, [512])
        oc1 = view(T + 4096, F32, [512])
        oc2 = view(T + 6144, F32, [512])
        N = slice(0, ncols)
        o, sq = r1, r2
        P.op("act", ACT(r1[:, N], ps[:, zb[0], N], AF.Ln), reads=[("ps", zb[0])], writes=["r1"])
        P.op("act", ACT(r2[:, N], ps[:, zb[1], N], AF.Ln), reads=[("ps", zb[1])], writes=["r2"])
        P.op("dve", CP(oc1[:, N], ps[:, ob[0], N]), reads=[("ps", ob[0])], writes=["oc1"])
        P.op("dve", CP(oc2[:, N], ps[:, ob[1], N]), reads=[("ps", ob[1])], writes=["oc2"])

        def stB():
            P.op("act", ACT(r1[:, N], r1[:, N], AF.Exp, scale=-1.0), reads=["r1"], writes=["r1"])
            P.op("act", ACT(r2[:, N], r2[:, N], AF.Exp, scale=-1.0), reads=["r2"], writes=["r2"])
            P.op("dve", TT(oc1[:, N], oc1[:, N], r1[:, N], ALU.mult), reads=["oc1", "r1"], writes=["oc1"])
            P.op("dve", TT(oc2[:, N], oc2[:, N], r2[:, N], ALU.mult), reads=["oc2", "r2"], writes=["oc2"])
            P.op("dve", STT(o[:, N], oc2[:, N], neglam[:, 0:1], oc1[:, N], ALU.mult, ALU.add),
                 reads=["oc1", "oc2", "r1"], writes=["r1"])
            defer(0, stB1)

        def stB1():
            P.op("act", ACT(sq[:, N], o[:, N], AF.Square), reads=["r1", "r2"], writes=["r2"])
            defer(0, stB2)

        def stB2():
            mbk = main_bank()
            P.op("pe", MM(ps[:, mbk, N], ones_d, sq[:, N]), reads=["r2"], writes=[("ps", mbk)])
            defer(0, lambda: stC(mbk))

        def stC(mbk):
            P.op("act", ACT(sq[:, N], ps[:, mbk, N], AF.Ln, bias=EPS, scale=1.0), reads=[("ps", mbk)],
                 writes=["r2"])
            P.op("act", ACT(sq[:, N], sq[:, N], AF.Exp, scale=-0.5), reads=["r2"], writes=["r2"])
            P.op("dve", STT(o[:, N], o[:, N], subg[:, 0:1], sq[:, N], ALU.mult, ALU.mult), reads=["r1", "r2"],
                 writes=["r1"])
            P.op("dve", TT(dst_fn(), o_view(o, ncols, dst_fn), silu_fn(), ALU.mult), reads=["r1"], writes=[keyw])
        defer(0, stB)

    def o_view(o, ncols, dst_fn):
        d = dst_fn()
        if len(d.shape) == 2:
            return o[:, 0:ncols]
        return o[:, 0:ncols].rearrange("p (h q) -> p h q", h=d.shape[1])

    def phaseA2(sg):
        hf = sg["hf"]
        for h in range(H):
            pb = 64 * (h % 2)
            blk = h // 2
            for qg in range(2):
                q0 = 512 * qg
                tq0 = 1024 * hf + q0
                ktl = [dict(kT=[kT_a[64 * m:64 * m + 64, h, 1024:1040] for m in range(2)], V=Vt_a[0:16, 8, h, :],
                            nk=16, c0=0, m0=16 + tq0)]
                for kt in range(8 * hf + 4 * qg + 4):
                    c0 = max(0, 128 * kt - tq0)
                    m0 = tq0 + c0 - 128 * kt
                    if kt < 8:
                        kTs = [kT_a[64 * m:64 * m + 64, h, 128 * kt:128 * kt + 128] for m in range(2)]
                        V = Vt_a[:, kt, h, :]
                    else:
                        kTs = [kT_b[64 * m:64 * m + 64, h, 128 * (kt - 8):128 * (kt - 8) + 128] for m in range(2)]
                        V = Vt_b[:, kt - 8, h, :]
                    ktl.append(dict(kT=kTs, V=V, nk=128, c0=c0, m0=m0, mask=(64 if 128 * kt >= tq0 else 0)))
                far = [t for t in ktl[1:] if t["c0"] == 0 and not t["mask"]]
                dg = [t for t in ktl[1:] if not (t["c0"] == 0 and not t["mask"])]
                order = [ktl[0]]
                if far:
                    step = max(1, len(far) // (len(dg) + 1))
                    fi = 0
                    for dt_ in dg:
                        order += far[fi:fi + step]
                        fi += step
                        order.append(dt_)
                    order += far[fi:]
                else:
                    order += dg
                assert len(order) == len(ktl)
                attn_job(h, lambda m, h=h, q0=q0: qT[64 * m:64 * m + 64, h, q0:q0 + 512], 512, order,
                         (4, 5), (6, 7))
                dst = lambda h=h, q0=q0: attnB[:, h, q0:q0 + 512]
                attn_finish(512, (4, 5), (6, 7), dst, dst, ("attnB", h, qg))

    def phaseA2s(sg):
        qs = view(SAMP + 4096, BF16, [8, 32])
        ks = view(SAMP + 4608, BF16, [8, 32])
        vns = view(SAMP, BF16, [2, 8, 128])
        for sq_ in range(2):
            for h in range(H):
                s = rslot("kc", 2)
                kcb = view(TMP + 2048 * s, BF16, [8, 128])
                vcb = view(TMP + 4096 + 2048 * s, BF16, [8, 128])
                P.dma("pool", kcb, cache_k[sq_, h].rearrange("(t p) f -> p t f", p=128), "kc%d" % s,
                      writes=[("kcb", s)])
                P.dma("pool", vcb, cache_v[sq_, h].rearrange("(t p) f -> p t f", p=128), "vc%d" % s,
                      writes=[("vcb", s)])
                kcT = view(TMP + 8192 + 2048 * s, BF16, [1024])
                for half in range(2):
                    bk = main_bank()
                    P.op("pe", [MM(ps[:, bk, 128 * k:128 * k + 128], kcb[:, 4 * half + k, :], identb)
                                for k in range(4)], reads=[("kcb", s)], writes=[("ps", bk)])
                    P.op("dve", CP(kcT[:, 512 * half:512 * half + 512], ps[:, bk, :]), reads=[("ps", bk)],
                         writes=[("kcT", s, half)])
                items = []
                for m in range(2):
                    sb = main_bank()
                    qap = qT[64 * m:64 * m + 64, h, 1040 + 16 * sq_:1056 + 16 * sq_]
                    grp = [MM(ps[:, sb, 16 * t:16 * t + 16], kcT[64 * m:64 * m + 64, 128 * t:128 * t + 128], qap,
                              start=True, stop=(t < 7)) for t in range(8)]
                    for hl in range(2):
                        grp.append(MM(ps[:, sb, 112:128], identb, Ehl[:, hl, h, 128:144], start=False, stop=(hl == 1)))
                    grp.append(MM(ps[0:16, sb, 128:144], ksamp[64 * m:64 * m + 64, h, 16 * sq_:16 * sq_ + 16], qap,
                                  start=True, stop=False))
                    for hl in range(2):
                        grp.append(MM(ps[0:16, sb, 128:144], identb[0:16, 0:16], Ehl[0:16, hl, h, 0:16], start=False,
                                      stop=(hl == 1)))
                    P.op("pe", grp, reads=[("kcT", s, 0), ("kcT", s, 1)], writes=[("ps", sb)])
                    sp_ = rslot("pts", 4)
                    pt = view(R4b + 1024 * sp_, BF16, [512])
                    key = ("pt", sp_)
                    P.op("act", ACT(pt[:, 0:128], ps[:, sb, 0:128], AF.Exp, bias=c15[:, h:h + 1], scale=SCALE),
                         reads=[("ps", sb)], writes=[key])
                    P.op("act", ACT(pt[0:16, 128:144], ps[0:16, sb, 128:144], AF.Exp, bias=c15[0:16, h:h + 1],
                                    scale=SCALE), reads=[("ps", sb)], writes=[key])
                    items.append((m, pt, key))
                for (m, pt, key) in items:
                    oc = (m * 8 + h) * 16
                    g1 = [MM(ps[:, 4, oc:oc + 16], vcb[:, t, :], pt[:, 16 * t:16 * t + 16], start=(t == 0), stop=False)
                          for t in range(8)]
                    g1.append(MM(ps[:, 4, oc:oc + 16], vns[0:16, sq_, h, :], pt[0:16, 128:144], start=False,
                                 stop=True))
                    g2 = [MM(ps[:, 5, oc:oc + 16], onesb, pt[:, 16 * t:16 * t + 16], start=(t == 0), stop=False)
                          for t in range(8)]
                    g2.append(MM(ps[:, 5, oc:oc + 16], onesb[0:16, :], pt[0:16, 128:144], start=False, stop=True))
                    P.op("pe", g1 + g2, reads=[key, ("vcb", s), "vns"], writes=[("ps", 4), ("ps", 5)])
            T = R4b + 6144
            r1 = view(T, F32, [512])
            o = view(T + 4096, F32, [512])
            sq = view(T + 6144, F32, [512])
            P.op("act", ACT(r1[:, 0:256], ps[:, 5, 0:256], AF.Ln), reads=[("ps", 5)], writes=["r1"])
            P.op("act", ACT(r1[:, 0:256], r1[:, 0:256], AF.Exp, scale=-1.0), reads=["r1"], writes=["r1"])
            P.op("dve", TT(r1[:, 0:256], ps[:, 4, 0:256], r1[:, 0:256], ALU.mult), reads=[("ps", 4), "r1"],
                 writes=["r1"])
            P.op("dve", STT(o[:, 0:128], r1[:, 128:256], neglam[:, 0:1], r1[:, 0:128], ALU.mult, ALU.add),
                 reads=["r1"], writes=["o"])
            P.op("act", ACT(sq[:, 0:128], o[:, 0:128], AF.Square), reads=["o"], writes=["sq"])
            mbk = main_bank()
            P.op("pe", MM(ps[:, mbk, 0:128], ones_d, sq[:, 0:128]), reads=["sq"], writes=[("ps", mbk)])
            P.op("act", ACT(sq[:, 0:128], ps[:, mbk, 0:128], AF.Ln, bias=EPS, scale=1.0), reads=[("ps", mbk)],
                 writes=["sq"])
            P.op("act", ACT(sq[:, 0:128], sq[:, 0:128], AF.Exp, scale=-0.5), reads=["sq"], writes=["sq"])
            P.op("dve", STT(o[:, 0:128], o[:, 0:128], subg[:, 0:1], sq[:, 0:128], ALU.mult, ALU.mult),
                 reads=["o", "sq"], writes=["o"])
            dst = attnB[:, :, 1040 + 16 * sq_:1056 + 16 * sq_]
            P.op("dve", TT(dst, o[:, 0:128].rearrange("p (h q) -> p h q", h=8), dst, ALU.mult), reads=["o"],
                 writes=[("attnBs", sq_)])

    ksamp = view(SAMP + 5120, BF16, [8, 32])

    def phaseB(sg):
        b, hf = sg["b"], sg["hf"]
        samp = sg["samp"]
        ncol = 1024 + sg["next"]
        atmp = view(R4, F32, [NCMAX])
        sgt = [view(R4 + 4352 + 2048 * i, F32, [512]) for i in range(2)]
        gl32p = view(R4 + 8448, F32, [8, 30])
        gl32s = view(R4 + 8448 + 960, F32, [8, 32])
        gs = view(R4 + 8448 + 960 + 1024, BF16, [8, 2, 46])
        stt_ = view(TMP + 2048, F32, [2048])
        cv = view(R4b, F32, [8, 512])
        if hf == 0:
            P.op("dve", MS(glu[:, :, 0:14], 0.0), writes=["gluh"])
            if not sg["next"]:
                P.op("dve", CP(glu[:, :, 14:30], gm), writes=["glum_all"])
        else:
            P.op("dve", CP(glu[:, :, 0:30], gh), writes=["gluh"])
        if samp:
            for sq_ in range(2):
                P.dma("sp", stt_[0:30, 1024 * sq_:1024 * sq_ + 1024], state_conv[sq_], "stt%d" % sq_, writes=[("stt", sq_)])
                P.dma("sp", conv_sample[sq_, 0:14, :], state_conv[sq_, 16:30, :], "cst")
                bk = main_bank()
                P.op("pe", [TR(ps[:, bk, 32 * g:32 * g + 30], stt_[0:30, 1024 * sq_ + 128 * g:1024 * sq_ + 128 * g + 128],
                               ident[0:30, 0:30]) for g in range(8)], reads=[("stt", sq_)], writes=[("ps", bk)])
                P.op("dve", CP(gs[:, :, sq_, 0:30], ps[:, bk, 0:256].rearrange("p (g j) -> p g j", g=8)[:, :, 0:30]),
                     reads=[("ps", bk)], writes=[("gs", sq_)])
        for (typ, g, col) in B_blocks:
            wap, wkey = next_w()
            for ci, (c0, n) in enumerate(coltiles(sg)):
                bk = inproj_group(wap, wkey, c0, n)
                if typ == "a":
                    P.op("act", ACT(atmp[:, c0:c0 + n], ps[:, bk, 0:n], AF.Copy), reads=[("ps", bk)],
                         writes=[("atmp", ci)])
                elif typ == "g":
                    s = rslot("sg", 2)
                    sg_ = sgt[s]
                    P.op("act", ACT(sg_[:, 0:n], ps[:, bk, 0:n], AF.Sigmoid), reads=[("ps", bk)], writes=[("sg", s)])
                    if ci < 2:
                        P.op("dve", TT(glu[:, g, 30 + c0:30 + c0 + n], atmp[:, c0:c0 + n], sg_[:, 0:n], ALU.mult),
                             reads=[("atmp", ci), ("sg", s)], writes=[("glu", g, ci)])
                        if ci == 1 and hf == 1:
                            P.op("dve", TT(gl32p[:, g, :], atmp[:, 994:1024], sg_[:, 482:512], ALU.mult),
                                 reads=[("atmp", ci), ("sg", s)], writes=[("gl32p", g)])
                        if ci == 1 and hf == 0:
                            P.op("dve", CP(gh[:, g, :], glu[:, g, 1024:1054]), reads=[("glu", g, ci)],
                                 writes=[("gh", g)])
                    else:
                        if hf == 0:
                            P.op("dve", TT(glu[:, g, 14:30], atmp[:, 1024:1040], sg_[:, 0:16], ALU.mult),
                                 reads=[("atmp", ci), ("sg", s)], writes=[("glum", g)])
                            P.op("dve", CP(gm[:, g, :], glu[:, g, 14:30]), reads=[("glum", g)], writes=[("gm", g)])
                        if samp:
                            P.op("dve", TT(gl32s[:, g, :], atmp[:, 1040:1072], sg_[:, 16:48], ALU.mult),
                                 reads=[("atmp", ci), ("sg", s)], writes=[("gl32s", g)])
                            P.op("dve", CP(gs[:, g, :, 30:46], gl32s[:, g, :].rearrange("p (s t) -> p s t", s=2)),
                                 reads=[("gl32s", g)], writes=[("gsn", g)])
                else:
                    P.op("act", ACT(convB[:, g, c0:c0 + n], ps[:, bk, 0:n], AF.Silu), reads=[("ps", bk)],
                         writes=[("convB", g, ci)])
        flush()
        P.barrier()
        if hf == 1:
            ost = view(TMP + 2048, F32, [1024])
            for half in range(2):
                bk = main_bank()
                P.op("pe", [TR(ps[0:30, bk, 128 * k:128 * k + 128], gl32p[:, 4 * half + k, :], ident) for k in range(4)],
                     reads=[("gl32p", 4 * half + k) for k in range(4)], writes=[("ps", bk)])
                P.op("dve", CP(ost[0:30, 512 * half:512 * half + 512], ps[0:30, bk, :]), reads=[("ps", bk)],
                     writes=[("ost", half)])
            P.dma("sp", conv_prompt[b], ost[0:30, :], "ost", reads=[("ost", 0), ("ost", 1)], writes=[("ost", 0), ("ost", 1)])
        if samp:
            ost2 = view(TMP + 6144, F32, [1024])
            for sq_ in range(2):
                for half in range(2):
                    bk = main_bank()
                    P.op("pe", [TR(ps[0:16, bk, 128 * k:128 * k + 128], gl32s[:, 4 * half + k, 16 * sq_:16 * sq_ + 16],
                                   ident) for k in range(4)],
                         reads=[("gl32s", 4 * half + k) for k in range(4)], writes=[("ps", bk)])
                    P.op("dve", CP(ost2[0:16, 512 * half:512 * half + 512], ps[0:16, bk, :]), reads=[("ps", bk)],
                         writes=[("ost2", half)])
                P.dma("sp", conv_sample[sq_, 14:30, :], ost2[0:16, :], "ost2",
                      reads=[("ost2", 0), ("ost2", 1)], writes=[("ost2", 0), ("ost2", 1)])
        ctl = [(0, 512, False), (512, 512, False)]
        if samp:
            ctl.append((1040, 32, True))
        T2 = R4
        dgs = [view(TMP + 256 * i, BF16, [128]) for i in range(8)]
        cvs = [view(R4b + 8192 * i, BF16, [8, 512]) for i in range(2)]
        mean = view(T2 + 4096, F32, [512])
        rstd = view(T2 + 6144, F32, [512])
        tnm = [view(T2 + 12032 + 2048 * i, F32, [512]) for i in range(2)]

        def conv_ct(idx, after_group=None):
            c0, n, is_s = ctl[idx]
            cv = cvs[idx % 2]
            for g in range(8):
                if after_group is not None and g > 0:
                    after_group(g - 1)
                bk = main_bank()
                bk2 = main_bank() if is_s else None
                for j in range(CW):
                    s = rslot("dg", 8)
                    P.op("dve", TS(dgs[s], identb, cw[:, g, j:j + 1], ALU.mult), writes=[("dg", s)])
                    if not is_s:
                        P.op("pe", MM(ps[:, bk, 0:n], dgs[s], glu[:, g, c0 + j:c0 + j + n], start=(j == 0),
                                      stop=(j == CW - 1)), reads=[("dg", s)], writes=[("ps", bk)])
                    else:
                        P.op("pe", [MM(ps[:, (bk, bk2)[q_], 0:16], dgs[s], gs[:, g, q_, j:j + 16], start=(j == 0),
                                       stop=(j == CW - 1)) for q_ in range(2)],
                             reads=[("dg", s)], writes=[("ps", bk), ("ps", bk2)])
                if not is_s:
                    P.op("act", ACT(cv[:, g, 0:n], ps[:, bk, 0:n], AF.Identity, bias=cb[:, g:g + 1], scale=1.0),
                         reads=[("ps", bk)], writes=[("cv", idx % 2, g)])
                else:
                    for q_ in range(2):
                        P.op("act", ACT(cv[:, g, 16 * q_:16 * q_ + 16], ps[:, (bk, bk2)[q_], 0:16], AF.Identity,
                                        bias=cb[:, g:g + 1], scale=1.0),
                             reads=[("ps", (bk, bk2)[q_])], writes=[("cv", idx % 2, g)])
            if after_group is not None:
                after_group(7)

        def stats_ct(idx):
            c0, n, is_s = ctl[idx]
            cv = cvs[idx % 2]
            for g in range(8):
                s = rslot("sqf", 2)
                sqf = view(T2 + 1024 * s, BF16, [512])
                P.op("act", ACT(sqf[:, 0:n], cv[:, g, 0:n], AF.Square), reads=[("cv", idx % 2, g)], writes=[("sqf", s)])
                P.op("pe", MM(ps[:, 4, 0:n], ones_lnb, cv[:, g, 0:n], start=(g == 0), stop=(g == 7)),
                     reads=[("cv", idx % 2, g)], writes=[("ps", 4)])
                P.op("pe", MM(ps[:, 5, 0:n], ones_lnb, sqf[:, 0:n], start=(g == 0), stop=(g == 7)),
                     reads=[("sqf", s)], writes=[("ps", 5)])

        def norm_a(idx):
            c0, n, is_s = ctl[idx]
            P.op("dve", CP(mean[:, 0:n], ps[:, 4, 0:n]), reads=[("ps", 4)], writes=["mean"])
            P.op("dve", TT(rstd[:, 0:n], mean[:, 0:n], mean[:, 0:n], ALU.mult), reads=["mean"], writes=["rstd"])
            P.op("dve", TT(rstd[:, 0:n], ps[:, 5, 0:n], rstd[:, 0:n], ALU.subtract), reads=[("ps", 5), "rstd"],
                 writes=["rstd"])
            P.op("act", ACT(rstd[:, 0:n], rstd[:, 0:n], AF.Ln, bias=EPS, scale=1.0), reads=["rstd"], writes=["rstd"])
            P.op("act", ACT(rstd[:, 0:n], rstd[:, 0:n], AF.Exp, scale=-0.5), reads=["rstd"], writes=["rstd"])

        def norm_b(idx, g):
            c0, n, is_s = ctl[idx]
            cv = cvs[idx % 2]
            if True:
                s = rslot("tnm", 2)
                t = tnm[s]
                P.op("pool", TT(t[:, 0:n], cv[:, g, 0:n], mean[:, 0:n], ALU.subtract),
                     reads=[("cv", idx % 2, g), "mean"], writes=[("tnm", s)])
                P.op("pool", TT(t[:, 0:n], t[:, 0:n], rstd[:, 0:n], ALU.mult), reads=[("tnm", s), "rstd"],
                     writes=[("tnm", s)])
                P.op("act", ACT(t[:, 0:n], t[:, 0:n], AF.Silu, bias=lnb[:, g:g + 1], scale=lng[:, g:g + 1]),
                     reads=[("tnm", s)], writes=[("tnm", s)])
                P.op("pool", TT(convB[:, g, c0:c0 + n], t[:, 0:n], convB[:, g, c0:c0 + n], ALU.mult),
                     reads=[("tnm", s)], writes=[("convBo", g, idx)])

        for idx in range(len(ctl)):
            if idx > 0:
                conv_ct(idx, after_group=lambda g, i=idx - 1: norm_b(i, g))
            else:
                conv_ct(idx)
            stats_ct(idx)
            norm_a(idx)
        early_gates(sg)
        for g in range(8):
            norm_b(len(ctl) - 1, g)

    def gate_blocks(sg):
        ctl = [(0, 512), (512, 512)]
        if sg["samp"]:
            ctl.append((1040, 32))
        sgc = view(R3, F32, [NCMAX])
        sga = view(R3 + 4352, F32, [NCMAX])
        wgc, kgc = next_w()
        for (c0, n) in ctl:
            b1 = inproj_group(wgc, kgc, c0, n)
            P.op("act", ACT(sgc[:, c0:c0 + n], ps[:, b1, 0:n], AF.Sigmoid), reads=[("ps", b1)],
                 writes=[("sgc", c0)])
        wga, kga = next_w()
        for (c0, n) in ctl:
            b2 = inproj_group(wga, kga, c0, n)
            P.op("act", ACT(sga[:, c0:c0 + n], ps[:, b2, 0:n], AF.Sigmoid), reads=[("ps", b2)],
                 writes=[("sga", c0)])

    def early_gates(sg):
        if EARLY_GATES:
            gate_blocks(sg)

    def phaseC(sg):
        ctl = [(0, 512), (512, 512)]
        if sg["samp"]:
            ctl.append((1040, 32))
        sgc = view(R3, F32, [NCMAX])
        sga = view(R3 + 4352, F32, [NCMAX])
        m1s = [view(R3 + 8704 + 2048 * i, F32, [512]) for i in range(2)]
        m2s = [view(R3 + 12800 + 2048 * i, F32, [512]) for i in range(2)]
        for f in range(16):
            if not (f == 0 and EARLY_GATES):
                gate_blocks(sg)
            wbo, kbo = next_w()
            for (c0, n) in ctl:
                s = rslot("pc", 2)
                m1, m2 = m1s[s], m2s[s]
                b3 = inproj_group(wbo, kbo, c0, n, nk=8, rhs=lambda kc: convB[:, kc, c0:c0 + n], koff=0)
                P.op("dve", TT(m1[:, 0:n], ps[:, b3, 0:n], sgc[:, c0:c0 + n], ALU.mult),
                     reads=[("ps", b3), ("sgc", c0)], writes=[("m1", s)])
                b4 = inproj_group(wbo, kbo, c0, n, nk=8, rhs=lambda kc: attnB[:, kc, c0:c0 + n], koff=8)
                P.op("dve", TT(m2[:, 0:n], ps[:, b4, 0:n], sga[:, c0:c0 + n], ALU.mult),
                     reads=[("ps", b4), ("sga", c0)], writes=[("m2", s)])
                P.op("dve", TT(merged[:, f, c0:c0 + n], m1[:, 0:n], m2[:, 0:n], ALU.add), reads=[("m1", s), ("m2", s)],
                     writes=[("merged", f, c0)])

    def phaseD(sg):
        b, hf = sg["b"], sg["hf"]
        wstate["wout_ok"] = sg["si"]
        tl = [(i, 128) for i in range(8)]
        if sg["samp"]:
            tl.append((8, 32))
        xr = [view(R5 + 2048 * i, F32, [512]) for i in range(3)]
        ys = [view(R5 + 6144 + 2048 * i, F32, [512]) for i in range(3)]

        def xsrc(ti, cg):
            if ti < 8:
                t0 = 1024 * hf + 128 * ti
                return [(slice(0, 128), x_prompt[b, t0:t0 + 128, 512 * cg:512 * cg + 512])]
            return [(slice(16 * q_, 16 * q_ + 16), x_sample[q_, :, 512 * cg:512 * cg + 512]) for q_ in range(2)]

        def ydst(ti, cg):
            if ti < 8:
                t0 = 1024 * hf + 128 * ti
                return [(slice(0, 128), y_prompt[b, t0:t0 + 128, 512 * cg:512 * cg + 512])]
            return [(slice(16 * q_, 16 * q_ + 16), y_sample[q_, :, 512 * cg:512 * cg + 512]) for q_ in range(2)]

        jobs = [(cg, ti, n) for cg in range(4) for (ti, n) in tl]

        def load(idx):
            cg, ti, n = jobs[idx]
            s = idx % 3
            for (sl, src) in xsrc(ti, cg):
                P.dma("sp", xr[s][sl, :], src, "xr%d" % s, writes=[("xr", s)])
        load(0)
        wcur = None
        nxt = []
        if OVERLAP0 and sg["si"] + 1 < len(segs) and not stop_here(sg):
            nxt = phase0_steps(segs[sg["si"] + 1], overlapped=True)
        for idx, (cg, ti, n) in enumerate(jobs):
            if nxt and idx % 2 == 1:
                nxt.pop(0)()
            if idx + 1 < len(jobs):
                load(idx + 1)
            if ti == 0:
                wcur = next_w()
            wap, wkey = wcur
            s = idx % 3
            c0 = 128 * ti if ti < 8 else 1040
            bk = main_bank()
            P.op("pe", [MM(ps[0:n, bk, :], merged[:, kc, c0:c0 + n], wap[:, kc, :], start=(kc == 0), stop=(kc == 15))
                        for kc in range(16)], reads=[wkey, "mergedall"], writes=[("ps", bk)])
            P.op("dve", TT(ys[s][0:n, :], ps[0:n, bk, :], xr[s][0:n, :], ALU.add), reads=[("ps", bk), ("xr", s)],
                 writes=[("ys", s)])
            for (sl, dst) in ydst(ti, cg):
                P.dma("sp", dst, ys[s][sl, :], "ys%d" % s, reads=[("ys", s)], writes=[("ys", s)])
        while nxt:
            nxt.pop(0)()

    stop = os.environ.get("KSTOP", "")
    phases = [("0", phase0), ("A1", phaseA1), ("A2", phaseA2), ("A2s", phaseA2s), ("B", phaseB), ("C", phaseC),
              ("D", phaseD)]
    done = False
    for sg in segs:
        for (pn, fn) in phases:
            if pn == "A2s" and not sg["samp"]:
                continue
            fn(sg)
            flush()
            P.barrier()
            if stop == "%d:%s" % (sg["si"], pn):
                done = True
                break
        if done:
            break
    P.barrier(full=True)
    return nc, P


_CACHE = {}


def _get_program():
    if "nc" not in _CACHE:
        from contextlib import ExitStack
        stack = ExitStack()
        nc, P = build_program(stack)
        block = stack.enter_context(nc.Block())
        P.emit(block)
        stack.close()
        _CACHE["nc"] = nc
    return _CACHE["nc"]


def kernel(x_prompt, x_sample, cache_k, cache_v, state_conv, meta_tokens, rel_bias, norm_gain, w_in,
           q_norm_gain, k_norm_gain, lambda_qk, subln_gain, conv_w, conv_b, conv_ln_gain, conv_ln_bias,
           w_branch_out, w_out):
    f = lambda a: np.ascontiguousarray(np.asarray(a, dtype=np.float32))
    nc = _get_program()
    oh = np.zeros((32, 383), np.float32)
    for s in range(383):
        oh[_bucket(s - 255), s] = 1.0
    shared = {
        "meta_tokens": f(meta_tokens), "rel_bias": f(rel_bias), "norm_gain": f(norm_gain).reshape(1, D),
        "w_in": f(w_in).reshape(D, DIN), "q_norm_gain": f(q_norm_gain).reshape(H, 64),
        "k_norm_gain": f(k_norm_gain).reshape(H, 64), "lambda_qk": f(lambda_qk).reshape(1, 256),
        "subln_gain": f(subln_gain).reshape(128, 1), "conv_w": f(conv_w).reshape(CW, 1024),
        "conv_b": f(conv_b).reshape(1024), "conv_ln_gain": f(conv_ln_gain).reshape(1024),
        "conv_ln_bias": f(conv_ln_bias).reshape(1024), "w_branch_out": f(w_branch_out).reshape(2048, D),
        "w_out": f(w_out).reshape(D, D), "c_ident": np.eye(128, dtype=np.float32), "c_oh": oh,
    }
    xp, xs = f(x_prompt), f(x_sample)
    ck, cvv, sc = f(cache_k)[0], f(cache_v)[0], f(state_conv)[0]
    in_maps = []
    import os
    ncores = int(os.environ.get('KCORES', NCORES))
    for c in range(ncores):
        m = dict(shared)
        m["x_prompt"] = xp[2 * c:2 * c + 2]
        m["x_sample"] = xs[2 * c:2 * c + 2]
        m["cache_k"] = ck[2 * c:2 * c + 2]
        m["cache_v"] = cvv[2 * c:2 * c + 2]
        m["state_conv"] = sc[2 * c:2 * c + 2]
        in_maps.append(m)
    res = run_bass_kernel_spmd(nc, in_maps, core_ids=list(range(ncores)))
    R = res.results
    cat = lambda k: np.concatenate([np.asarray(r[k], dtype=np.float32) for r in R], axis=0)
    return (cat("y_prompt"), cat("y_sample"), cat("k_prompt")[None], cat("v_prompt")[None], cat("conv_prompt")[None],
            cat("k_sample")[None], cat("v_sample")[None], cat("conv_sample")[None])
```
